# Optimizing a Trainium2 kernel written in Bass

```python
import math
import jax, jax.numpy as jnp
from jax import lax
import numpy as np

D_MODEL = 1024
BATCH = 8
SEQ = 4096
DEPTH = 4
DEC_BATCH = 8
DEC_SEQ = 32
PAST_LEN = 2048

CHUNK = 64
Q_BLOCK = 128
N_MIXERS = 2
SB_HEADS = 16
SB_HEAD_DIM = D_MODEL // SB_HEADS
DIFF_HEADS = 8
DIFF_HEAD_DIM = D_MODEL // (2 * DIFF_HEADS)
ROPE_DIM = DIFF_HEAD_DIM // 4
ROPE_THETA = 500000.0
D_FF = 4 * D_MODEL
LN_EPS = 1e-5
SUBLN_EPS = 1e-5
DEEPNORM_ALPHA = (2 * DEPTH) ** 0.25
DEEPNORM_BETA = (8 * DEPTH) ** -0.25
N_SB_LAYERS = (DEPTH + 1) // 2
N_DIFF_LAYERS = DEPTH // 2
NEG_INF = float(np.finfo(np.float32).min)

kernel_name = "stickbreak_diffattn_deepnorm_stream_step"


def _layer_norm(x, g, b):
    xf = x.astype(jnp.float32)
    mu = jnp.mean(xf, axis=-1, keepdims=True)
    var = jnp.mean(jnp.square(xf - mu), axis=-1, keepdims=True)
    return ((xf - mu) * lax.rsqrt(var + LN_EPS) * g + b).astype(x.dtype)


def _rope_partial(x, pos):
    half = ROPE_DIM // 2
    inv_freq = ROPE_THETA ** (-jnp.arange(0, ROPE_DIM, 2, dtype=jnp.float32) / ROPE_DIM)
    ang = pos.astype(jnp.float32)[:, None] * inv_freq[None, :]
    cos = jnp.cos(ang)[None, :, None, None, :]
    sin = jnp.sin(ang)[None, :, None, None, :]
    xr = x[..., :ROPE_DIM].astype(jnp.float32)
    x1, x2 = xr[..., :half], xr[..., half:]
    rot = jnp.concatenate([x1 * cos - x2 * sin, x2 * cos + x1 * sin], axis=-1)
    return jnp.concatenate([rot.astype(x.dtype), x[..., ROPE_DIM:]], axis=-1)


def _sweep_query_blocks(fn, q, q_pos):
    B, S = q.shape[0], q.shape[1]
    qb = min(Q_BLOCK, S)
    nb = S // qb
    if nb == 1:
        return fn(q, q_pos)
    qs = jnp.moveaxis(q.reshape((B, nb, qb) + q.shape[2:]), 1, 0)
    ps = q_pos.reshape(nb, qb)
    out = lax.map(lambda a: fn(a[0], a[1]), (qs, ps))
    out = jnp.moveaxis(out, 0, 1)
    return out.reshape((B, S) + out.shape[3:])


def _stick_breaking_block(q, q_pos, k, v, k_pos):
    z = jnp.einsum('bqhd,bkhd->bhqk', q, k, preferred_element_type=jnp.float32) * (SB_HEAD_DIM ** -0.5)
    mask = k_pos[None, :] < q_pos[:, None]
    log_keep = jnp.where(mask, jax.nn.log_sigmoid(-z), 0.0)
    after = lax.cumsum(log_keep, axis=3, reverse=True) - log_keep
    w = jnp.where(mask, jnp.exp(jax.nn.log_sigmoid(z) + after), 0.0)
    return jnp.einsum('bhqk,bkhd->bqhd', w.astype(v.dtype), v)


def _diff_block(q, q_pos, k, v, k_pos, lam):
    s = jnp.einsum('bqhcd,bkhcd->bhcqk', q, k, preferred_element_type=jnp.float32) * (DIFF_HEAD_DIM ** -0.5)
    mask = (k_pos[None, :] // CHUNK) <= (q_pos[:, None] // CHUNK)
    p = jax.nn.softmax(jnp.where(mask, s, NEG_INF), axis=-1)
    a = p[:, :, 0] - lam * p[:, :, 1]
    return jnp.einsum('bhqk,bkhe->bqhe', a.astype(v.dtype), v)


def _stick_breaking_mixer(x, pos, k_past, v_past, w_qkv, w_o):
    B, S, _ = x.shape
    qkv = (x @ w_qkv).reshape(B, S, 3, SB_HEADS, SB_HEAD_DIM)
    q, k, v = qkv[:, :, 0], qkv[:, :, 1], qkv[:, :, 2]
    if k_past is None:
        k_all, v_all, k_pos = k, v, pos
    else:
        k_all = jnp.concatenate([k_past.astype(k.dtype), k], axis=1)
        v_all = jnp.concatenate([v_past.astype(v.dtype), v], axis=1)
        k_pos = jnp.concatenate([jnp.arange(k_past.shape[1], dtype=jnp.int32), pos])
    o = _sweep_query_blocks(lambda qb, pb: _stick_breaking_block(qb, pb, k_all, v_all, k_pos), q, pos)
    return o.reshape(B, S, D_MODEL) @ w_o, k, v


def _diff_mixer(x, pos, k_past, v_past, w_qkv, lq1, lk1, lq2, lk2, subln_g, w_o, lambda_init):
    B, S, _ = x.shape
    qkv = (x @ w_qkv).reshape(B, S, 3, DIFF_HEADS, 2, DIFF_HEAD_DIM)
    q = _rope_partial(qkv[:, :, 0], pos)
    k = _rope_partial(qkv[:, :, 1], pos)
    v = qkv[:, :, 2].reshape(B, S, DIFF_HEADS, 2 * DIFF_HEAD_DIM)
    lam = (jnp.exp(jnp.sum(lq1.astype(jnp.float32) * lk1.astype(jnp.float32)))
           - jnp.exp(jnp.sum(lq2.astype(jnp.float32) * lk2.astype(jnp.float32))) + lambda_init)
    if k_past is None:
        k_all, v_all, k_pos = k, v, pos
    else:
        k_all = jnp.concatenate([k_past.astype(k.dtype), k], axis=1)
        v_all = jnp.concatenate([v_past.astype(v.dtype), v], axis=1)
        k_pos = jnp.concatenate([jnp.arange(k_past.shape[1], dtype=jnp.int32), pos])
    o = _sweep_query_blocks(lambda qb, pb: _diff_block(qb, pb, k_all, v_all, k_pos, lam), q, pos)
    of = o.astype(jnp.float32)
    of = of * lax.rsqrt(jnp.mean(jnp.square(of), axis=-1, keepdims=True) + SUBLN_EPS) * subln_g
    o = (of * (1.0 - lambda_init)).astype(x.dtype)
    return o.reshape(B, S, D_MODEL) @ w_o, k, v


def _sq_relu_mlp(x, w_up, w_down):
    h = jax.nn.relu(x @ w_up)
    return (h * h) @ w_down


def _trunk(x, pos, sb_k_past, sb_v_past, diff_k_past, diff_v_past,
           sb_w_qkv, sb_w_o, diff_w_qkv, diff_lambda_q1, diff_lambda_k1, diff_lambda_q2, diff_lambda_k2,
           diff_subln_g, diff_w_o, ln1_g, ln1_b, mlp_w_up, mlp_w_down, ln2_g, ln2_b):
    sb_k, sb_v, df_k, df_v = [], [], [], []
    for i in range(DEPTH):
        j = i // N_MIXERS
        if i % N_MIXERS == 0:
            kp = None if sb_k_past is None else sb_k_past[j]
            vp = None if sb_v_past is None else sb_v_past[j]
            mix, kn, vn = _stick_breaking_mixer(x, pos, kp, vp, sb_w_qkv[j], sb_w_o[j])
            sb_k.append(kn)
            sb_v.append(vn)
        else:
            kp = None if diff_k_past is None else diff_k_past[j]
            vp = None if diff_v_past is None else diff_v_past[j]
            lambda_init = 0.8 - 0.6 * math.exp(-0.3 * i)
            mix, kn, vn = _diff_mixer(x, pos, kp, vp, diff_w_qkv[j], diff_lambda_q1[j], diff_lambda_k1[j],
                                      diff_lambda_q2[j], diff_lambda_k2[j], diff_subln_g[j], diff_w_o[j],
                                      lambda_init)
            df_k.append(kn)
            df_v.append(vn)
        x = _layer_norm(DEEPNORM_ALPHA * x + mix, ln1_g[i], ln1_b[i])
        x = _layer_norm(DEEPNORM_ALPHA * x + _sq_relu_mlp(x, mlp_w_up[i], mlp_w_down[i]), ln2_g[i], ln2_b[i])
    return x, jnp.stack(sb_k), jnp.stack(sb_v), jnp.stack(df_k), jnp.stack(df_v)


def setup_inputs(seed: int = 0) -> dict:
    key = jax.random.key(seed)
    ks = jax.random.split(key, 24)
    D = D_MODEL

    def nrm(k, shape, scale):
        return scale * jax.random.normal(k, shape, jnp.float32)

    w_in = D ** -0.5
    return {
        "x_prompt": nrm(ks[0], (BATCH, SEQ, D), 1.0),
        "x_sample": nrm(ks[1], (DEC_BATCH, DEC_SEQ, D), 1.0),
        "cache_sb_k": nrm(ks[2], (N_SB_LAYERS, DEC_BATCH, PAST_LEN, SB_HEADS, SB_HEAD_DIM), 1.0),
        "cache_sb_v": nrm(ks[3], (N_SB_LAYERS, DEC_BATCH, PAST_LEN, SB_HEADS, SB_HEAD_DIM), DEEPNORM_BETA),
        "cache_diff_k": nrm(ks[4], (N_DIFF_LAYERS, DEC_BATCH, PAST_LEN, DIFF_HEADS, 2, DIFF_HEAD_DIM), 1.0),
        "cache_diff_v": nrm(ks[5], (N_DIFF_LAYERS, DEC_BATCH, PAST_LEN, DIFF_HEADS, 2 * DIFF_HEAD_DIM), DEEPNORM_BETA),
        "sb_w_qkv": jnp.concatenate([nrm(ks[6], (N_SB_LAYERS, D, 2 * D), w_in),
                                     nrm(ks[7], (N_SB_LAYERS, D, D), DEEPNORM_BETA * w_in)], axis=-1),
        "sb_w_o": nrm(ks[8], (N_SB_LAYERS, D, D), DEEPNORM_BETA * w_in),
        "diff_w_qkv": jnp.concatenate([nrm(ks[9], (N_DIFF_LAYERS, D, 2 * D), w_in),
                                       nrm(ks[10], (N_DIFF_LAYERS, D, D), DEEPNORM_BETA * w_in)], axis=-1),
        "diff_lambda_q1": nrm(ks[11], (N_DIFF_LAYERS, DIFF_HEAD_DIM), 0.1),
        "diff_lambda_k1": nrm(ks[12], (N_DIFF_LAYERS, DIFF_HEAD_DIM), 0.1),
        "diff_lambda_q2": nrm(ks[13], (N_DIFF_LAYERS, DIFF_HEAD_DIM), 0.1),
        "diff_lambda_k2": nrm(ks[14], (N_DIFF_LAYERS, DIFF_HEAD_DIM), 0.1),
        "diff_subln_g": 1.0 + nrm(ks[15], (N_DIFF_LAYERS, 2 * DIFF_HEAD_DIM), 0.02),
        "diff_w_o": nrm(ks[16], (N_DIFF_LAYERS, D, D), DEEPNORM_BETA * w_in),
        "ln1_g": 1.0 + nrm(ks[17], (DEPTH, D), 0.02),
        "ln1_b": nrm(ks[18], (DEPTH, D), 0.02),
        "mlp_w_up": nrm(ks[19], (DEPTH, D, D_FF), w_in),
        "mlp_w_down": nrm(ks[20], (DEPTH, D_FF, D), DEEPNORM_BETA * D_FF ** -0.5),
        "ln2_g": 1.0 + nrm(ks[21], (DEPTH, D), 0.02),
        "ln2_b": nrm(ks[22], (DEPTH, D), 0.02),
    }


def reference(x_prompt, x_sample, cache_sb_k, cache_sb_v, cache_diff_k, cache_diff_v,
              sb_w_qkv, sb_w_o, diff_w_qkv, diff_lambda_q1, diff_lambda_k1, diff_lambda_q2, diff_lambda_k2,
              diff_subln_g, diff_w_o, ln1_g, ln1_b, mlp_w_up, mlp_w_down, ln2_g, ln2_b):
    weights = (sb_w_qkv, sb_w_o, diff_w_qkv, diff_lambda_q1, diff_lambda_k1, diff_lambda_q2, diff_lambda_k2,
               diff_subln_g, diff_w_o, ln1_g, ln1_b, mlp_w_up, mlp_w_down, ln2_g, ln2_b)
    pos_p = jnp.arange(x_prompt.shape[1], dtype=jnp.int32)
    y_prompt, sbk_p, sbv_p, dfk_p, dfv_p = _trunk(x_prompt, pos_p, None, None, None, None, *weights)
    past_len = cache_sb_k.shape[2]
    pos_s = past_len + jnp.arange(x_sample.shape[1], dtype=jnp.int32)
    y_sample, sbk_s, sbv_s, dfk_s, dfv_s = _trunk(x_sample, pos_s, cache_sb_k, cache_sb_v,
                                                  cache_diff_k, cache_diff_v, *weights)
    return (y_prompt, y_sample, sbk_p, sbv_p, dfk_p, dfv_p, sbk_s, sbv_s, dfk_s, dfv_s)
```

```python
import math
import contextlib
import numpy as np
import concourse.bass as bass
import concourse.mybir as mybir
from concourse.bass_utils import run_bass_kernel_spmd

F32 = mybir.dt.float32
BF16 = mybir.dt.bfloat16
AF = mybir.ActivationFunctionType
ALU = mybir.AluOpType

D = 1024
DFF = 4096
DS = 32
NCH = 8
LN_EPS = 1e-5
SUBLN_EPS = 1e-5
DEPTH = 4
ALPHA = (2 * DEPTH) ** 0.25
ROPE_THETA = 500000.0

COMPUTE = ("pe", "act", "dve", "pool")
N_FG = 24
N_BG = 0


class Sched:
    def __init__(self, same_engine_sync=True):
        self.ops = {e: [] for e in ("pe", "act", "dve", "pool", "sp")}
        self.last_w = {}
        self.readers = {}
        self.nfg = 0
        self.nbg = 0
        self.sem_count = [0] * (N_FG + N_BG)
        self.same_engine_sync = same_engine_sync
        self.barrier_deps = []
        self.barrier_id = 0
        self.eng_barrier = {e: 0 for e in self.ops}

    def barrier(self):
        deps = []
        for e in COMPUTE:
            if self.ops[e]:
                for i in range(len(self.ops[e]) - 1, -1, -1):
                    if self.ops[e][i]["me"][0] == "c":
                        deps.append(("c", e, i))
                        break
        for s in range(N_FG):
            if self.sem_count[s] > 0:
                deps.append(("d", s, self.sem_count[s]))
        self.barrier_deps = deps
        self.barrier_id += 1

    def _add(self, eng, fn, reads, writes, dma=None, bg=False):
        deps = []
        writes = list(writes) + [k for k in reads if k.startswith("ps")]
        reads = [k for k in reads if not k.startswith("ps")]
        for k in reads:
            w = self.last_w.get(k)
            if w is not None:
                deps.append(w)
        for k in writes:
            w = self.last_w.get(k)
            if w is not None:
                deps.append(w)
            deps.extend(self.readers.get(k, ()))
        if self.eng_barrier[eng] != self.barrier_id:
            deps.extend(self.barrier_deps)
            self.eng_barrier[eng] = self.barrier_id
        idx = len(self.ops[eng])
        if dma is None:
            me = ("c", eng, idx)
        else:
            if bg:
                s = N_FG + self.nbg % N_BG
                self.nbg += 1
            else:
                s = self.nfg % N_FG
                self.nfg += 1
            prev = self.sem_count[s]
            if prev > 0:
                deps.append(("d", s, prev))
            self.sem_count[s] = prev + 1
            me = ("d", s, prev + 1)
        fdeps = []
        for d in deps:
            if d[0] == "c" and d[1] == eng:
                if eng == "pe" or not self.same_engine_sync:
                    continue
                if d[2] == idx:
                    continue
            fdeps.append(d)
        self.ops[eng].append({"fn": fn, "deps": fdeps, "me": me, "sig": False})
        for k in reads:
            self.readers.setdefault(k, []).append(me)
        for k in writes:
            self.last_w[k] = me
            self.readers[k] = []
        return me

    def op(self, eng, fn, reads=(), writes=()):
        return self._add(eng, fn, reads, writes)

    def dma(self, q, out, in_, reads=(), writes=(), bg=False, **kw):
        return self._add(q, (out, in_, kw), reads, writes, dma=True, bg=bg)

    def emit(self, nc):
        for e, lst in self.ops.items():
            for o in lst:
                for d in o["deps"]:
                    if d[0] == "c":
                        self.ops[d[1]][d[2]]["sig"] = True
        rank = {}
        for e, lst in self.ops.items():
            c = 0
            for i, o in enumerate(lst):
                if o["sig"]:
                    c += 1
                    rank[(e, i)] = c
        with contextlib.ExitStack() as st:
            csem = {e: st.enter_context(nc.semaphore("cs_" + e)) for e in COMPUTE}
            dsem = [st.enter_context(nc.semaphore("ds_%d" % i)) for i in range(N_FG + N_BG)]
            block = st.enter_context(nc.Block())

            def run(eng_name):
                def body(eng):
                    seen = {}
                    for i, o in enumerate(self.ops[eng_name]):
                        need = {}
                        for d in o["deps"]:
                            if d[0] == "c":
                                key = ("c", d[1])
                                val = rank[(d[1], d[2])]
                            else:
                                key = ("d", d[1])
                                val = 16 * d[2]
                            if val > seen.get(key, 0):
                                need[key] = max(need.get(key, 0), val)
                        for key, val in need.items():
                            sem = csem[key[1]] if key[0] == "c" else dsem[key[1]]
                            eng.wait_ge(sem, val)
                            seen[key] = val
                        me = o["me"]
                        if me[0] == "d":
                            out, in_, kw = o["fn"]
                            eng.dma_start(out=out, in_=in_, **kw).then_inc(dsem[me[1]], 16)
                        else:
                            ins = o["fn"](eng)
                            if o["sig"]:
                                ins.then_inc(csem[eng_name], 1)
                    if eng_name == "sp":
                        for s in range(N_FG + N_BG):
                            if self.sem_count[s] > 0:
                                eng.wait_ge(dsem[s], 16 * self.sem_count[s])
                return body

            block.sync(run("sp"))
            block.tensor(run("pe"))
            block.scalar(run("act"))
            block.vector(run("dve"))
            block.gpsimd(run("pool"))


class Alloc:
    def __init__(self, nc, base, limit=229344):
        self.nc, self.off, self.limit = nc, base, limit
        self.n = 0

    def t(self, name, shape, dt):
        sz = 2 if dt == BF16 else 4
        nb = sz
        for s in shape[1:]:
            nb *= s
        nb = (nb + 63) // 64 * 64
        Alloc.cnt = getattr(Alloc, "cnt", 0) + 1
        h = self.nc.alloc_sbuf_tensor_at("%s_%d" % (name, Alloc.cnt), list(shape), dt, offset=self.off)
        self.off += nb
        assert self.off <= self.limit, ("SBUF overflow", name, self.off)
        return h


class Builder:
    def __init__(self, SP, SC, NL, same_engine_sync=True, debug=False):
        self.debug = debug
        self.SP, self.SC, self.NL = SP, SC, NL
        self.T = SP + DS
        self.nc = bass.Bass("TRN2", target_bir_lowering=False)
        self.S = Sched(same_engine_sync)
        self.uid = 0

    def mm(self, out, lhsT, rhs, start, stop, reads, writes, tp=None):
        kw = {}
        if tp is not None:
            kw["tile_position"] = tp
        self.S.op("pe", lambda e: e.matmul(out, lhsT=lhsT, rhs=rhs, start=start, stop=stop,
                                            skip_group_check=True, **kw), reads, writes)

    def tr(self, out, in_, ident, reads, writes):
        self.S.op("pe", lambda e: e.transpose(out=out, in_=in_, identity=ident), reads, writes)

    def act(self, out, in_, func, reads, writes, scale=1.0, bias=0.0):
        self.S.op("act", lambda e: e.activation(out=out, in_=in_, func=func, bias=bias, scale=scale),
                  reads, writes)

    def cp(self, eng, out, in_, reads, writes):
        if eng == "act":
            self.S.op("act", lambda e: e.copy(out=out, in_=in_), reads, writes)
        else:
            self.S.op(eng, lambda e: e.tensor_copy(out=out, in_=in_), reads, writes)

    def tt(self, eng, out, in0, in1, op, reads, writes):
        self.S.op(eng, lambda e: e.tensor_tensor(out=out, in0=in0, in1=in1, op=op), reads, writes)

    def ts(self, eng, out, in0, s1, s2, op0, op1, reads, writes):
        if s2 is None:
            self.S.op(eng, lambda e: e.tensor_scalar(out=out, in0=in0, scalar1=s1, scalar2=None, op0=op0),
                      reads, writes)
        else:
            self.S.op(eng, lambda e: e.tensor_scalar(out=out, in0=in0, scalar1=s1, scalar2=s2, op0=op0, op1=op1),
                      reads, writes)

    def stt(self, eng, out, in0, scalar, in1, op0, op1, reads, writes):
        self.S.op(eng, lambda e: e.scalar_tensor_tensor(out=out, in0=in0, scalar=scalar, in1=in1, op0=op0, op1=op1),
                  reads, writes)

    def memset(self, eng, ap, val, writes):
        self.S.op(eng, lambda e: e.memset(ap, val), (), writes)

    def recip(self, out, in_, reads, writes):
        self.S.op("dve", lambda e: e.reciprocal(out=out, in_=in_), reads, writes)

    def dma(self, out, in_, reads=(), writes=(), q="sp", **kw):
        self.S.dma(q, out, in_, reads, writes, **kw)

    def declare(self):
        nc, SP, SC, T = self.nc, self.SP, self.SC, self.T
        I = lambda n, s: nc.dram_tensor(n, list(s), F32, kind="ExternalInput").ap()
        O = lambda n, s: nc.dram_tensor(n, list(s), F32, kind="ExternalOutput").ap()
        Sc = lambda n, s, dt: nc.dram_tensor(n, list(s), dt, kind=("ExternalOutput" if self.debug else "Internal")).ap()
        self.xp = I("xp", [SP, D]); self.xs = I("xs", [DS, D])
        self.csbk = I("csbk", [2, SC, D]); self.csbv = I("csbv", [2, SC, D])
        self.cdfk = I("cdfk", [2, SC, D]); self.cdfv = I("cdfv", [2, SC, D])
        self.sb_wqkv = I("sb_wqkv", [2, D, 3 * D]); self.sb_wo = I("sb_wo", [2, D, D])
        self.df_wqkv = I("df_wqkv", [2, D, 3 * D]); self.df_wo = I("df_wo", [2, D, D])
        self.lq1 = I("lq1", [2, 64]); self.lk1 = I("lk1", [2, 64])
        self.lq2 = I("lq2", [2, 64]); self.lk2 = I("lk2", [2, 64])
        self.subg = I("subg", [2, 128])
        self.ln1g = I("ln1g", [4, D]); self.ln1b = I("ln1b", [4, D])
        self.ln2g = I("ln2g", [4, D]); self.ln2b = I("ln2b", [4, D])
        self.wup = I("wup", [4, D, DFF]); self.wdn = I("wdn", [4, DFF, D])
        self.consts = I("consts", [128, 7 * 128])
        self.rope = I("rope", [T, 16])
        self.yp = O("yp", [SP, D]); self.ys = O("ys", [DS, D])
        self.o_sbk_p = O("o_sbk_p", [2, SP, D]); self.o_sbv_p = O("o_sbv_p", [2, SP, D])
        self.o_dfk_p = O("o_dfk_p", [2, SP, D]); self.o_dfv_p = O("o_dfv_p", [2, SP, D])
        self.o_sbk_s = O("o_sbk_s", [2, DS, D]); self.o_sbv_s = O("o_sbv_s", [2, DS, D])
        self.o_dfk_s = O("o_dfk_s", [2, DS, D]); self.o_dfv_s = O("o_dfv_s", [2, DS, D])
        self.xT_f = Sc("xT_f", [NCH, 128, T], F32)
        self.xT_b = Sc("xT_b", [NCH, 128, T], BF16)
        self.qT_s = Sc("qT_s", [NCH, 128, T], BF16)
        self.kT_p = Sc("kT_p", [NCH, 128, SP], BF16)
        self.kT_c = Sc("kT_c", [NCH, 128, SC + DS], BF16)
        self.v_p = Sc("v_p", [SP, D], BF16)
        self.v_c = Sc("v_c", [SC + DS, D], BF16)
        self.oT_s = Sc("oT_s", [NCH, 128, T], BF16)
        NL = self.NL
        self.wqkv_b = Sc("wqkv_b", [NL, 128, NCH, 3 * D], BF16)
        self.wo_b = Sc("wo_b", [NL, 2, 128, NCH, 512], BF16)
        self.wup_b = Sc("wup_b", [NL, 8, 128, NCH, 512], BF16)
        self.wdn_b = Sc("wdn_b", [NL, 8, 128, 32, 128], BF16)

    def cast_steps(self, l, only=None, engs=("pool",)):
        j = l // 2
        wq = (self.sb_wqkv if l % 2 == 0 else self.df_wqkv)[j]
        wo = (self.sb_wo if l % 2 == 0 else self.df_wo)[j].rearrange("(kc p) (ob c) -> ob p kc c", p=128, c=512)
        wu = self.wup[l].rearrange("(kc p) (fb c) -> fb p kc c", p=128, c=512)
        wd = self.wdn[l].rearrange("(f p) (oc c) -> oc p f c", p=128, c=128)
        steps = []

        def mk(kind, i):
            def go():
                b = self.stg_n % 2
                self.stg_n += 1
                sf, sb_ = self.stg_f[b], self.stg_b[b]
                fk, bk = "stgf%d" % b, "stgb%d" % b
                n = 4096
                if kind == "wq":
                    self.dma(sf[:, 0:3 * D], wq[i * 128:(i + 1) * 128, :], writes=[fk])
                    n = 3 * D
                    dst = self.wqkv_b[l][:, i, :]
                elif kind == "wo":
                    self.dma(sf[:, :].rearrange("p (k c) -> p k c", c=512), wo[i], writes=[fk])
                    dst = self.wo_b[l, i].rearrange("p k c -> p (k c)")
                elif kind == "wu":
                    self.dma(sf[:, :].rearrange("p (k c) -> p k c", c=512), wu[i], writes=[fk])
                    dst = self.wup_b[l, i].rearrange("p k c -> p (k c)")
                else:
                    s3 = sf[:, :].rearrange("p (f c) -> p f c", c=128)
                    for q4 in range(4):
                        self.dma(s3[:, q4 * 8:(q4 + 1) * 8, :], wd[i][:, q4 * 8:(q4 + 1) * 8, :], writes=[fk + "_%d" % q4])
                    dst = self.wdn_b[l, i].rearrange("p f c -> p (f c)")
                rk = [fk] if kind != "wd" else [fk + "_%d" % q for q in range(4)]
                self.cp(engs[self.stg_n % len(engs)], sb_[:, 0:n], sf[:, 0:n], rk, [bk] + rk)

                def store():
                    self.dma(dst, sb_[:, 0:n], reads=[bk], writes=["%s%d_%d" % (kind, l, i)])
                return store
            return go
        for kind, cnt in (("wq", 8), ("wo", 2), ("wu", 8), ("wd", 8)):
            if only is not None and kind not in only:
                continue
            for i in range(cnt):
                steps.append(mk(kind, i))
        return steps

    def alloc_staging(self, A):
        self.stg_f = [A.t("stgf", [128, 4096], F32) for _ in range(2)]
        self.stg_b = [A.t("stgb", [128, 4096], BF16) for _ in range(2)]
        self.stg_n = 0

    def prologue(self):
        nc, S = self.nc, self.S
        A = Alloc(nc, 16512)
        self.cf = A.t("cf", [128, 7 * 128], F32)
        self.cb = A.t("cb", [128, 7 * 128], BF16)
        cb = self.cb
        self.ident_b = cb[:, 0:128]
        self.negtri = cb[:, 128:256]
        self.negones = cb[:, 256:384]
        self.ones_b = cb[:, 384:512]
        self.mask_sb = cb[:, 512:640]
        self.mask_df = cb[:, 640:768]
        self.onesm = cb[:, 768:896]
        self.ident_f = self.cf[:, 0:128]
        self.lnp = A.t("lnp", [128, 4, 4, NCH], F32)
        self.lam = A.t("lam", [128, 2, 8], F32)
        self.lamv = A.t("lamv", [128, 2, 4, 64], F32)
        self.ones1024 = A.t("ones1024", [128, 128], BF16)
        self.wq_sb = A.t("wq_sb", [128, NCH, 3 * D], BF16)
        self.persist_end = A.off
        self.pp = [nc.alloc_psum_tensor("psp%d" % i, [128, 2, 512], F32) for i in range(4)]
        self.ps = [self.pp[i // 2][:, i % 2, :] for i in range(8)]

        self.dma(self.cf[:], self.consts, writes=["cf"])
        self.cp("dve", cb[:], self.cf[:], ["cf"], ["cb"])
        self.ts("dve", self.ones1024[:], self.cf[:, 384:512], 1.0 / 1024.0, None, ALU.mult, None, ["cf"], ["cb"])
        for l in range(self.NL):
            for i, src in enumerate((self.ln1g, self.ln1b, self.ln2g, self.ln2b)):
                self.dma(self.lnp[:, l, i, :], src[l].rearrange("(c p) -> p c", p=128), writes=["lnp"],
                         allow_slow_non_contiguous=True)
        for j in range(self.NL // 2):
            l = 2 * j + 1
            lam_init = 0.8 - 0.6 * math.exp(-0.3 * l)
            for i, src in enumerate((self.lq1, self.lk1, self.lq2, self.lk2)):
                self.dma(self.lamv[:, j, i:i + 1, :], src[j:j + 1, :].partition_broadcast(128), writes=["lamv"])
            self.dma(self.lam[:, j, 4:5], self.subg[j].rearrange("(p o) -> p o", o=1), writes=["lamg"],
                     allow_slow_non_contiguous=True)
            lv = self.lamv
            self.tt("dve", lv[:, j, 0, :], lv[:, j, 0, :], lv[:, j, 1, :], ALU.mult, ["lamv"], ["lamv"])
            self.tt("dve", lv[:, j, 2, :], lv[:, j, 2, :], lv[:, j, 3, :], ALU.mult, ["lamv"], ["lamv"])
            self.S.op("dve", (lambda jj: lambda e: e.reduce_sum(out=self.lam[:, jj, 0:1], in_=lv[:, jj, 0, :],
                                                               axis=mybir.AxisListType.X))(j), ["lamv"], ["lam"])
            self.S.op("dve", (lambda jj: lambda e: e.reduce_sum(out=self.lam[:, jj, 1:2], in_=lv[:, jj, 2, :],
                                                               axis=mybir.AxisListType.X))(j), ["lamv"], ["lam"])
            self.act(self.lam[:, j, 0:2], self.lam[:, j, 0:2], AF.Exp, ["lam"], ["lam"])
            self.tt("dve", self.lam[:, j, 2:3], self.lam[:, j, 1:2], self.lam[:, j, 0:1], ALU.subtract, ["lam"], ["lam"])
            self.ts("dve", self.lam[:, j, 3:4], self.lam[:, j, 2:3], -lam_init, None, ALU.add, None, ["lam"], ["lam"])
            self.ts("dve", self.lam[:, j, 5:6], self.lam[:, j, 4:5], 1.0 - lam_init, None, ALU.mult, None,
                    ["lamg", "lam"], ["lam"])
        A2 = Alloc(nc, self.persist_end)
        self.alloc_staging(A2)
        for st_ in self.cast_steps(0, only=("wq",), engs=("dve", "act")):
            st_()()
        self.pending_casts = self.cast_steps(0, only=("wo", "wu", "wd"))
        for l in range(1, self.NL):
            self.pending_casts += self.cast_steps(l)
        xin = [A2.t("xin", [128, D], F32) for _ in range(2)]
        xtf = [A2.t("xtf", [128, NCH, 128], F32) for _ in range(2)]
        xtb = [A2.t("xtb", [128, NCH, 128], BF16) for _ in range(2)]
        tiles = [(self.xp, i * 128, 128, i * 128) for i in range(self.SP // 128)] + [(self.xs, 0, DS, self.SP)]
        xT_f = self.xT_f.rearrange("c p t -> p c t")
        xT_b = self.xT_b.rearrange("c p t -> p c t")
        pend_st = []
        for n, (src, r0, nt, t0) in enumerate(tiles):
            b = n % 2
            self.dma(xin[b][:nt, :], src[r0:r0 + nt, :], writes=["xin%d" % b])
            for half in range(2):
                bank = self.ps[(2 * n + half) % 8]
                key = "ps%d" % ((2 * n + half) % 8)
                for cc in range(4):
                    c = half * 4 + cc
                    self.tr(bank[:, cc * 128:cc * 128 + nt], xin[b][:nt, c * 128:(c + 1) * 128],
                            self.ident_f[:nt, :nt], ["xin%d" % b, "cf"], [key])
                src_v = bank[:].rearrange("p (c t) -> p c t", t=128)[:, :, 0:nt]
                self.cp("dve", xtf[b][:, half * 4:half * 4 + 4, 0:nt], src_v, [key], ["xtf%d" % b])
                self.cp("act", xtb[b][:, half * 4:half * 4 + 4, 0:nt], src_v, [key], ["xtb%d" % b])
            for (o_, i_, rk_) in pend_st:
                self.dma(o_, i_, reads=rk_)
            pend_st = [(xT_f[:, :, t0:t0 + nt], xtf[b][:, :, 0:nt], ["xtf%d" % b]),
                       (xT_b[:, :, t0:t0 + nt], xtb[b][:, :, 0:nt], ["xtb%d" % b])]
        for (o_, i_, rk_) in pend_st:
            self.dma(o_, i_, reads=rk_)
        self.S.barrier()

    def load_wqkv(self, l):
        for kc in range(NCH):
            self.dma(self.wq_sb[:, kc, :], self.wqkv_b[l][:, kc, :], reads=["wq%d_%d" % (l, kc)], writes=["wq_sb"])

    def phase1(self, l):
        nc, SP, SC = self.nc, self.SP, self.SC
        diff = (l % 2 == 1)
        j = l // 2
        A = Alloc(nc, self.persist_end)
        xb = [A.t("xb", [128, NCH, 512], BF16) for _ in range(2)]
        qTg = [A.t("qTg", [128, NCH, 512], BF16) for _ in range(2)]
        kTg = [A.t("kTg", [128, NCH, 512], BF16) for _ in range(2)]
        kvf = [A.t("kvf", [128, 2 * D], F32) for _ in range(2)]
        qf = [A.t("qf", [128, D], F32) for _ in range(2)] if diff else None
        qb = [A.t("qb", [128, D], BF16) for _ in range(2)]
        kb = [A.t("kb", [128, D], BF16) for _ in range(2)]
        vb = [A.t("vb", [128, D], BF16) for _ in range(2)]
        if diff:
            rt = [A.t("rt", [128, 16], F32) for _ in range(2)]
            rtmp = [A.t("rtmp", [128, 4, 16, 8], F32) for _ in range(2)]
            rtmpq = [A.t("rtmpq", [128, 4, 16, 8], F32) for _ in range(2)]
        kcin = [A.t("kcin", [128, D], BF16) for _ in range(2)]
        kcT = [A.t("kcT", [128, NCH, 128], BF16) for _ in range(2)]
        xT_b = self.xT_b.rearrange("c p t -> p c t")
        qT_s = self.qT_s.rearrange("c p t -> p c t")
        kT_p = self.kT_p.rearrange("c p t -> p c t")
        kT_c = self.kT_c.rearrange("c p t -> p c t")
        if diff:
            Kp, Vp, Ks, Vs = self.o_dfk_p[j], self.o_dfv_p[j], self.o_dfk_s[j], self.o_dfv_s[j]
            ck = self.cdfk[j]
        else:
            Kp, Vp, Ks, Vs = self.o_sbk_p[j], self.o_sbv_p[j], self.o_sbk_s[j], self.o_sbv_s[j]
            ck = self.csbk[j]
        cstg = [A.t("cstg", [128, D], F32) for _ in range(2)]
        cvb = [A.t("cvb", [128, D], BF16) for _ in range(2)]
        cv = (self.cdfv if diff else self.csbv)[j]

        pend = {"cur": [], "prev": []}

        def defer(out, in_, rkeys):
            pend["cur"].append((out, in_, rkeys))

        def rotate():
            for (o_, i_, rk_) in pend["prev"]:
                self.dma(o_, i_, reads=rk_)
            pend["prev"] = pend["cur"]
            pend["cur"] = []

        def cache_tile(ci):
            b = ci % 2
            self.dma(cstg[0][:, :], ck[ci * 128:(ci + 1) * 128, :], writes=["cstg0"])
            self.cp("dve", kcin[b][:, :], cstg[0][:, :], ["cstg0"], ["kcin%d" % b])
            self.dma(cstg[1][:, :], cv[ci * 128:(ci + 1) * 128, :], writes=["cstg1"])
            self.cp("act", cvb[b][:, :], cstg[1][:, :], ["cstg1"], ["cvb%d" % b])
            defer(self.v_c[ci * 128:(ci + 1) * 128, :], cvb[b][:, :], ["cvb%d" % b])
            bk = 4 + b
            bank, key = self.ps[bk], "ps%d" % bk
            bview = bank[:].bitcast(BF16)
            for c in range(NCH):
                self.tr(bview[:, c * 128:(c + 1) * 128], kcin[b][:, c * 128:(c + 1) * 128], self.ident_b,
                        ["kcin%d" % b, "cb"], [key])
            self.cp("dve" if b == 0 else "act", kcT[b][:, :, :], bview.rearrange("p (c t) -> p c t", t=128),
                    [key], ["kcT%d" % b])
            defer(kT_c[:, :, ci * 128:(ci + 1) * 128], kcT[b][:, :, :], ["kcT%d" % b])

        groups = [(g * 512, 512, False) for g in range(SP // 512)] + [(SP, DS, True)]
        cache_done = 0
        tcnt = 0
        bankc = 0
        for gi, (g0, NT, is_s) in enumerate(groups):
            gb = gi % 2
            self.dma(xb[gb][:, :, 0:NT], xT_b[:, :, g0:g0 + NT], writes=["xb%d" % gb])
            ntile = max(1, NT // 128)
            for ti in range(ntile):
                nt = min(128, NT)
                c0 = ti * 128
                tb = tcnt % 2
                tcnt += 1
                tok = g0 + c0
                for nb in range(6):
                    bk = bankc % 4
                    bankc += 1
                    bank, key = self.ps[bk], "ps%d" % bk
                    for kc in range(NCH):
                        self.mm(bank[:nt, :], xb[gb][:, kc, c0:c0 + nt], self.wq_sb[:, kc, nb * 512:(nb + 1) * 512],
                                kc == 0, kc == NCH - 1, ["xb%d" % gb, "wq_sb"], [key])
                    if nb < 2:
                        if diff:
                            self.cp("act", qf[tb][:nt, nb * 512:(nb + 1) * 512], bank[:nt, :], [key], ["qf%d" % tb])
                        else:
                            self.act(qb[tb][:nt, nb * 512:(nb + 1) * 512], bank[:nt, :], AF.Copy, [key],
                                     ["qb%d" % tb], scale=0.125)
                    else:
                        eng = "dve" if nb % 2 == 0 else "act"
                        self.cp(eng, kvf[tb][:nt, (nb - 2) * 512:(nb - 1) * 512], bank[:nt, :], [key],
                                ["kvf%d_%d" % (tb, (nb - 2) // 2)])
                kkey, vkey = "kvf%d_0" % tb, "kvf%d_1" % tb
                if diff:
                    self.dma(rt[tb][:nt, :], self.rope[tok:tok + nt, :], writes=["rt%d" % tb])
                    cosb = rt[tb][:nt, 0:8].unsqueeze(1).to_broadcast([nt, 16, 8])
                    sinb = rt[tb][:nt, 8:16].unsqueeze(1).to_broadcast([nt, 16, 8])
                    for (buf, bkey, reng, tmp, rk) in ((qf[tb][:nt, :], "qf%d" % tb, "dve", rtmpq[tb], "rtmpq%d" % tb),
                                                       (kvf[tb][:nt, 0:D], kkey, "pool", rtmp[tb], "rtmp%d" % tb)):
                        v3 = buf.rearrange("p (g d) -> p g d", d=64)
                        x1, x2 = v3[:, :, 0:8], v3[:, :, 8:16]
                        self.tt(reng, tmp[:nt, 0], x1, cosb, ALU.mult, [bkey, "rt%d" % tb], [rk])
                        self.tt(reng, tmp[:nt, 1], x2, sinb, ALU.mult, [bkey, "rt%d" % tb], [rk])
                        self.tt(reng, tmp[:nt, 2], x2, cosb, ALU.mult, [bkey, "rt%d" % tb], [rk])
                        self.tt(reng, tmp[:nt, 3], x1, sinb, ALU.mult, [bkey, "rt%d" % tb], [rk])
                        self.tt(reng, x1, tmp[:nt, 0], tmp[:nt, 1], ALU.subtract, [rk], [bkey])
                        self.tt(reng, x2, tmp[:nt, 2], tmp[:nt, 3], ALU.add, [rk], [bkey])
                    self.act(qb[tb][:nt, :], qf[tb][:nt, :], AF.Copy, ["qf%d" % tb], ["qb%d" % tb], scale=0.125)
                if is_s:
                    defer(Ks[0:nt, :], kvf[tb][:nt, 0:D], [kkey])
                    defer(Vs[0:nt, :], kvf[tb][:nt, D:2 * D], [vkey])
                else:
                    defer(Kp[tok:tok + nt, :], kvf[tb][:nt, 0:D], [kkey])
                    defer(Vp[tok:tok + nt, :], kvf[tb][:nt, D:2 * D], [vkey])
                self.cp("dve", kb[tb][:nt, :], kvf[tb][:nt, 0:D], [kkey], ["kb%d" % tb])
                self.cp("act", vb[tb][:nt, :], kvf[tb][:nt, D:2 * D], [vkey], ["vb%d" % tb])
                if is_s:
                    defer(self.v_c[SC:SC + nt, :], vb[tb][:nt, :], ["vb%d" % tb])
                else:
                    defer(self.v_p[tok:tok + nt, :], vb[tb][:nt, :], ["vb%d" % tb])
                for (src, skey, dst, dkey, bk) in ((qb[tb], "qb%d" % tb, qTg[gb], "qTg%d" % gb, 4 + tb),
                                                   (kb[tb], "kb%d" % tb, kTg[gb], "kTg%d" % gb, 6 + tb)):
                    bank, key = self.ps[bk], "ps%d" % bk
                    bview = bank[:].bitcast(BF16)
                    for c in range(NCH):
                        self.tr(bview[:, c * 128:c * 128 + nt], src[:nt, c * 128:(c + 1) * 128],
                                self.ident_b[:nt, :nt], [skey, "cb"], [key])
                    eng = "dve" if bk < 6 else "act"
                    self.cp(eng, dst[:, :, c0:c0 + nt], bview.rearrange("p (c t) -> p c t", t=128)[:, :, 0:nt],
                            [key], [dkey])
                if cache_done < SC // 128:
                    cache_tile(cache_done)
                    cache_done += 1
                rotate()
            self.dma(qT_s[:, :, g0:g0 + NT], qTg[gb][:, :, 0:NT], reads=["qTg%d" % gb])
            if is_s:
                self.dma(kT_c[:, :, SC:SC + NT], kTg[gb][:, :, 0:NT], reads=["kTg%d" % gb])
            else:
                self.dma(kT_p[:, :, g0:g0 + NT], kTg[gb][:, :, 0:NT], reads=["kTg%d" % gb])
        while cache_done < SC // 128:
            cache_tile(cache_done)
            cache_done += 1
            rotate()
        rotate()
        rotate()
        self.S.barrier()

    def phase2(self, l):
        nc, SP, SC = self.nc, self.SP, self.SC
        diff = (l % 2 == 1)
        j = l // 2
        NKT = SP // 128
        NCT = SC // 128
        A = Alloc(nc, self.persist_end)
        kT = [A.t("kT", [128, SP], BF16) for _ in range(2)]
        qT = [A.t("qT", [128, SP], BF16) for _ in range(2)]
        vv = [A.t("vv", [128, NKT, 128], BF16) for _ in range(2)]
        kTc = [A.t("kTc", [128, SC + DS], BF16) for _ in range(2)]
        qTc = [A.t("qTc", [128, DS], BF16) for _ in range(2)]
        vc = [A.t("vc", [128, NCT + 1, 128], BF16) for _ in range(2)]
        W = [A.t("W", [128, 2, 512], BF16) for _ in range(3)]
        oTt = [A.t("oTt", [128, 512], BF16) for _ in range(2)]
        if diff:
            rec = A.t("rec", [128, 512], F32)
            rec2 = A.t("rec2", [128, 512], F32)
            recp = A.t("recp", [128, 2, 512], F32)
            o0 = A.t("o0", [128, 512], F32)
            o1 = A.t("o1", [128, 512], F32)
            osq = A.t("osq", [128, 512], BF16)
            lnv = A.t("lnv", [128, 512], F32)
        else:
            E = [A.t("E", [128, 2, 512], F32) for _ in range(2)]
            SPt = [A.t("SPt", [128, 2, 512], BF16) for _ in range(2)]
            Tt = [A.t("Tt", [128, 2, 512], F32) for _ in range(2)]
            R = [A.t("R", [128, 2, 512], F32) for _ in range(2)]
        qT_s, kT_p, kT_c, oT_s = self.qT_s, self.kT_p, self.kT_c, self.oT_s
        ps, pp = self.ps, self.pp
        mask = self.mask_df if diff else self.mask_sb

        def load_chunk(c):
            b = c % 2
            self.dma(kT[b][:, :], kT_p[c], writes=["kT%d" % b])
            self.dma(qT[b][:, :], qT_s[c][:, 0:SP], writes=["qT%d" % b])
            self.dma(vv[b][:, :, :], self.v_p[:, c * 128:(c + 1) * 128].rearrange("(t p) e -> p t e", p=128),
                     writes=["vv%d" % b])
            self.dma(kTc[b][:, :], kT_c[c], writes=["kTc%d" % b])
            self.dma(qTc[b][:, :], qT_s[c][:, SP:SP + DS], writes=["qTc%d" % b])
            self.dma(vc[b][:, 0:NCT, :], self.v_c[0:SC, c * 128:(c + 1) * 128].rearrange("(t p) e -> p t e", p=128),
                     writes=["vc_sb%d" % b])
            self.dma(vc[b][0:DS, NCT, :], self.v_c[SC:SC + DS, c * 128:(c + 1) * 128],
                     writes=["vc_sb%d" % b])

        tasks = []
        blocks = [(B * 512, 512, False) for B in range(SP // 512)] + [(0, DS, True)]
        for c in range(NCH):
            for bi, (q0, nq, is_s) in enumerate(blocks):
                tl = []
                if is_s:
                    tl.append(dict(kind="part", nk=DS, kt=NCT, col0=0, N=DS, masked=(not diff)))
                    for jt in range(NCT - 1, -1, -1):
                        tl.append(dict(kind="full", nk=128, kt=jt, col0=0, N=DS, masked=False))
                else:
                    B = q0 // 512
                    for jt in range(4 * B + 3, -1, -1):
                        if jt >= 4 * B:
                            i = jt - 4 * B
                            tl.append(dict(kind="diag", nk=128, kt=jt, col0=128 * i, N=512 - 128 * i, masked=True))
                        else:
                            tl.append(dict(kind="full", nk=128, kt=jt, col0=0, N=512, masked=False))
                for n, t in enumerate(tl):
                    t.update(c=c, bi=bi, q0=q0, nq=nq, is_s=is_s, first=(n == 0), last=(n == len(tl) - 1))
                    tasks.append(t)
        for n, t in enumerate(tasks):
            t["n"] = n

        def kq_pair(t):
            n = t["n"]
            zp, key = pp[n % 2], "psz%d" % (n % 2)
            b = t["c"] % 2
            nk, col0, N = t["nk"], t["col0"], t["N"]
            mw = min(128, N)
            if t["masked"]:
                for hp in range(2):
                    self.mm(zp[:nk, hp, col0:col0 + mw], self.ident_b[:nk, :nk], mask[:nk, :mw], True, False,
                            ["cb"], [key])
            for hp in range(2):
                pr = slice(hp * 64, hp * 64 + 64)
                if t["is_s"]:
                    kap = kTc[b][pr, t["kt"] * 128:t["kt"] * 128 + nk]
                    qap = qTc[b][pr, :]
                    rk = ["kTc%d" % b, "qTc%d" % b]
                else:
                    kap = kT[b][pr, t["kt"] * 128:t["kt"] * 128 + nk]
                    qap = qT[b][pr, t["q0"]:t["q0"] + 512]
                    rk = ["kT%d" % b, "qT%d" % b]
                self.mm(zp[:nk, hp, col0:col0 + N], kap, qap[:, col0:col0 + N], not t["masked"], True, rk, [key])

        def vap(t, lo, hi):
            b = t["c"] % 2
            if t["is_s"]:
                return vc[b][:t["nk"], t["kt"], lo:hi], "vc_sb%d" % b
            return vv[b][:t["nk"], t["kt"], lo:hi], "vv%d" % b

        def obank(t):
            return (t["c"] * len(blocks) + t["bi"]) % 2

        def sb_A(t):
            kq_pair(t)

        def sb_B(t):
            n, nk, col0, N = t["n"], t["nk"], t["col0"], t["N"]
            zp, key = pp[n % 2], "psz%d" % (n % 2)
            self.act(E[n % 2][:nk, :, :N], zp[:nk, :, col0:col0 + N], AF.Exp, [key], ["E%d" % (n % 2)])
            self.act(SPt[n % 2][:nk, :, :N], E[n % 2][:nk, :, :N], AF.Ln, ["E%d" % (n % 2)], ["SPt%d" % (n % 2)],
                     bias=1.0)

        def sb_C(t):
            n, nk, col0, N = t["n"], t["nk"], t["col0"], t["N"]
            zp, key = pp[n % 2], "psz%d" % (n % 2)
            for hp in range(2):
                self.mm(zp[:nk, hp, col0:col0 + N], self.negtri[:nk, :nk], SPt[n % 2][:nk, hp, :N], False, True,
                        ["cb", "SPt%d" % (n % 2)], [key])
            if not t["last"]:
                for hp in range(2):
                    self.mm(pp[2][:, hp, col0:col0 + N], self.negones[:nk, :], SPt[n % 2][:nk, hp, :N], True, True,
                            ["cb", "SPt%d" % (n % 2)], ["psc"])

        def sb_D(t):
            n, nk, col0, N = t["n"], t["nk"], t["col0"], t["N"]
            zp, key = pp[n % 2], "psz%d" % (n % 2)
            rb = obank(t)
            if t["first"]:
                self.memset("dve", R[rb][:, :, :], 0.0, ["R%d" % rb])
            self.tt("dve", Tt[n % 2][:nk, :, :N], zp[:nk, :, col0:col0 + N], R[rb][:nk, :, col0:col0 + N], ALU.add,
                    [key, "R%d" % rb], ["Tt%d" % (n % 2)])
            if not t["last"]:
                self.tt("dve", R[rb][:, :, col0:col0 + N], R[rb][:, :, col0:col0 + N], pp[2][:, :, col0:col0 + N],
                        ALU.add, ["psc", "R%d" % rb], ["R%d" % rb])

        def sb_E(t):
            n, nk, N = t["n"], t["nk"], t["N"]
            self.act(W[n % 3][:nk, :, :N], Tt[n % 2][:nk, :, :N], AF.Exp, ["Tt%d" % (n % 2)], ["W%d" % (n % 3)])

        def sb_F(t):
            n = t["n"]
            ob = 6 + obank(t)
            col0, N, nk = t["col0"], t["N"], t["nk"]
            for hp in range(2):
                va, vkey = vap(t, hp * 64, hp * 64 + 64)
                pr = slice(hp * 64, hp * 64 + 64)
                tp = (0, 64) if hp == 1 else None
                self.mm(ps[ob][pr, col0:col0 + N], va, W[n % 3][:nk, hp, :N], t["first"], t["last"],
                        [vkey, "W%d" % (n % 3)], ["ps%d" % ob], tp=tp)
            if t["last"]:
                self.finish_sb(t, ob, oTt, oT_s)

        def df_A(t):
            kq_pair(t)

        def df_B(t):
            n, nk, col0, N = t["n"], t["nk"], t["col0"], t["N"]
            zp, key = pp[n % 2], "psz%d" % (n % 2)
            self.act(W[n % 3][:nk, :, :N], zp[:nk, :, col0:col0 + N], AF.Exp, [key], ["W%d" % (n % 3)])

        def df_C(t):
            n, nk, col0, N = t["n"], t["nk"], t["col0"], t["N"]
            va, vkey = vap(t, 0, 128)
            wt, wk = W[n % 3], "W%d" % (n % 3)
            for hp in range(2):
                obk = 4 + hp
                self.mm(ps[obk][:, col0:col0 + N], va, wt[:nk, hp, :N], t["first"], t["last"], [vkey, wk],
                        ["ps%d" % obk])
            for hp in range(2):
                dbk = 6 + hp
                self.mm(ps[dbk][:, col0:col0 + N], self.ones_b[:nk, :], wt[:nk, hp, :N], t["first"], t["last"],
                        ["cb", wk], ["ps%d" % dbk])
            if t["last"]:
                nq = t["nq"]
                tb = obank(t)
                nlam = self.lam[:, j, 3:4]
                gp = self.lam[:, j, 5:6]
                msb, mkey = pp[0][:, 0, :], "psz0"
                self.act(recp[:, :, :nq], pp[3][:, :, :nq], AF.Ln, ["ps6", "ps7"], ["rec"])
                self.act(recp[:, :, :nq], recp[:, :, :nq], AF.Exp, ["rec"], ["rec"], scale=-1.0)
                self.tt("dve", o0[:, :nq], ps[4][:, :nq], recp[:, 0, :nq], ALU.mult, ["ps4", "rec"], ["o0"])
                self.tt("dve", o1[:, :nq], ps[5][:, :nq], recp[:, 1, :nq], ALU.mult, ["ps5", "rec"], ["o1"])
                self.stt("dve", o0[:, :nq], o1[:, :nq], nlam, o0[:, :nq], ALU.mult, ALU.add, ["o0", "o1", "lam"], ["o0"])
                self.tt("dve", osq[:, :nq], o0[:, :nq], o0[:, :nq], ALU.mult, ["o0"], ["osq"])
                col = (self.SP if t["is_s"] else t["q0"])
                cc_ = t["c"]

                def part2():
                    self.mm(msb[:, :nq], self.onesm, osq[:, :nq], True, True, ["cb", "osq"], [mkey])
                    self.act(lnv[:, :nq], msb[:, :nq], AF.Ln, [mkey], ["lnv"], bias=SUBLN_EPS)
                    self.act(lnv[:, :nq], lnv[:, :nq], AF.Exp, ["lnv"], ["lnv"], scale=-0.5)
                    self.stt("dve", oTt[tb][:, :nq], o0[:, :nq], gp, lnv[:, :nq], ALU.mult, ALU.mult,
                             ["o0", "lnv", "lam"], ["oTt%d" % tb])
                    self.dma(oT_s[cc_][:, col:col + nq], oTt[tb][:, :nq], reads=["oTt%d" % tb])
                fin2.append((n + 2, part2))

        nt_ = len(tasks)
        fin2 = []
        self.alloc_staging(A)
        if l == 0:
            ncast = min(len(self.pending_casts), 18 + 26)
        else:
            ncast = min(len(self.pending_casts), 26)
        interval = max(1, (nt_ - 8) // (ncast + 1))
        self.deferred_store = None
        self.casts_left = ncast

        def do_cast_step():
            if self.deferred_store is not None:
                self.deferred_store()
                self.deferred_store = None
            if self.casts_left > 0 and self.pending_casts:
                self.casts_left -= 1
                self.deferred_store = self.pending_casts.pop(0)()
        load_chunk(0)
        cur_c = -1
        first_i = 0
        for i in range(nt_ + 2):
            if i < nt_:
                t = tasks[i]
                if t["c"] != cur_c:
                    cur_c = t["c"]
                    first_i = i
                if i == first_i + 2:
                    if cur_c + 1 < NCH:
                        load_chunk(cur_c + 1)
                if i >= 4 and i % interval == 0:
                    do_cast_step()
                if diff:
                    df_A(t)
                    df_B(t)
                else:
                    sb_A(t)
                    sb_B(t)
            if 0 <= i - 1 < nt_:
                t = tasks[i - 1]
                if diff:
                    df_C(t)
                    while fin2 and fin2[0][0] <= t["n"]:
                        fin2.pop(0)[1]()
                else:
                    sb_C(t)
                    sb_D(t)
                    sb_E(t)
            if not diff and 0 <= i - 2 < nt_:
                sb_F(tasks[i - 2])
        while fin2:
            fin2.pop(0)[1]()
        while self.casts_left > 0 and self.pending_casts:
            do_cast_step()
        do_cast_step()
        self.S.barrier()

    def finish_sb(self, t, ob, oTt, oT_s):
        nq = t["nq"]
        tb = ob - 6
        self.cp("dve", oTt[tb][:, :nq], self.ps[ob][:, :nq], ["ps%d" % ob], ["oTt%d" % tb])
        col = (self.SP if t["is_s"] else t["q0"])
        self.dma(oT_s[t["c"]][:, col:col + nq], oTt[tb][:, :nq], reads=["oTt%d" % tb])

    def ln_gen(self, l, which, R, Rk, N, out_bf, out_key, vbt, sqt, sm):
        ps = self.ps
        self.cp("act", vbt[:, :, :N], R[:, :, :N], [Rk], ["vbt"]); yield
        self.act(sqt[:, :, :N], R[:, :, :N], AF.Square, [Rk], ["sqt"]); yield
        for kc in range(NCH):
            self.mm(ps[4][:, :N], self.ones1024[:], vbt[:, kc, :N], kc == 0, kc == NCH - 1, ["cb", "vbt"], ["ps4"])
        for kc in range(NCH):
            self.mm(ps[5][:, :N], self.ones1024[:], sqt[:, kc, :N], kc == 0, kc == NCH - 1, ["cb", "sqt"], ["ps5"])
        yield
        mean, m2, rstd, nmr = sm[0], sm[1], sm[2], sm[3]
        self.cp("dve", mean[:, :N], ps[4][:, :N], ["ps4"], ["sm0"]); yield
        self.tt("dve", m2[:, :N], mean[:, :N], mean[:, :N], ALU.mult, ["sm0"], ["sm1"])
        self.tt("dve", m2[:, :N], ps[5][:, :N], m2[:, :N], ALU.subtract, ["ps5", "sm1"], ["sm1"]); yield
        self.act(rstd[:, :N], m2[:, :N], AF.Ln, ["sm1"], ["sm2"], bias=LN_EPS)
        self.act(rstd[:, :N], rstd[:, :N], AF.Exp, ["sm2"], ["sm2"], scale=-0.5); yield
        self.stt("dve", nmr[:, :N], mean[:, :N], -1.0, rstd[:, :N], ALU.mult, ALU.mult, ["sm0", "sm2"], ["sm3"])
        yield
        gi = 0 if which == 1 else 2
        g = self.lnp[:, l, gi, :]
        b = self.lnp[:, l, gi + 1, :]
        half = NCH // 2
        for h in range(2):
            Rh = R[:, h * half:(h + 1) * half, :N]
            bc = lambda a: a[:, :N].unsqueeze(1).to_broadcast([128, half, N])
            self.tt("dve", Rh, Rh, bc(rstd), ALU.mult, [Rk, "sm2"], [Rk]); yield
            self.tt("dve", Rh, Rh, bc(nmr), ALU.add, [Rk, "sm3"], [Rk]); yield
            for kc in range(h * half, (h + 1) * half):
                self.act(R[:, kc, :N], R[:, kc, :N], AF.Identity, [Rk, "lnp"], [Rk],
                         scale=g[:, kc:kc + 1], bias=b[:, kc:kc + 1])
                if kc % 2 == 1:
                    yield
            self.cp("dve", out_bf[:, h * half:(h + 1) * half, :N], Rh, [Rk], [out_key]); yield

    def phase3(self, l):
        nc, SP = self.nc, self.SP
        last_layer = (l == self.NL - 1)
        A = Alloc(nc, self.persist_end)
        NR = 3
        ring = [A.t("ring", [128, 4096], BF16) for _ in range(NR)]
        oT = A.t("oT", [128, NCH, 512], BF16)
        Rb = [A.t("R", [128, NCH, 512], F32) for _ in range(2)]
        xbf = A.t("xbf", [128, NCH, 512], BF16)
        vbt = A.t("vbt", [128, NCH, 512], BF16)
        sqt = A.t("sqt", [128, NCH, 512], BF16)
        hT = A.t("hT", [128, 32, 512], BF16)
        sm = [A.t("sm", [128, 512], F32) for _ in range(4)]
        rl = [A.t("rl", [128, 512], F32) for _ in range(2)]
        yt = [A.t("yt", [128, D], F32) for _ in range(2)] if last_layer else None
        ps = self.ps
        xT_f = self.xT_f.rearrange("c p t -> p c t")
        xT_b = self.xT_b.rearrange("c p t -> p c t")
        oT_s = self.oT_s.rearrange("c p t -> p c t")
        groups = [(g * 512, 512) for g in range(SP // 512)] + [(SP, DS)]
        NG = len(groups)

        seq = [("wo", 0, 0), ("wo", 0, 1)]
        for gi in range(NG):
            seq += [("wu", gi, fb) for fb in range(8)]
            seq += [("wd", gi, 0), ("wd", gi, 1)]
            if gi + 1 < NG:
                seq += [("wo", gi + 1, 0), ("wo", gi + 1, 1)]
            seq += [("wd", gi, oc) for oc in range(2, 8)]
        pos_of = {k: i for i, k in enumerate(seq)}
        loaded = {}
        st = {"pos": 0, "acc": 0, "hc": 0}
        PRE = NR - 1

        def wsrc(kind, i):
            if kind == "wo":
                return self.wo_b[l, i].rearrange("p k c -> p (k c)"), ["wo%d_%d" % (l, i)]
            if kind == "wu":
                return self.wup_b[l, i].rearrange("p k c -> p (k c)"), ["wu%d_%d" % (l, i)]
            return self.wdn_b[l, i].rearrange("p f c -> p (f c)"), ["wd%d_%d" % (l, i)]

        def get_w(key):
            pos = pos_of[key]
            while st["pos"] <= pos + PRE and st["pos"] < len(seq):
                kind, _, i = seq[st["pos"]]
                src, rk = wsrc(kind, i)
                sl = st["pos"] % NR
                self.dma(ring[sl][:, :], src, reads=rk, writes=["ring%d" % sl])
                loaded[st["pos"]] = (ring[sl], "ring%d" % sl)
                st["pos"] += 1
            return loaded.pop(pos)

        def wo_part(gi):
            g0, N = groups[gi]
            R, Rk = Rb[gi % 2], "R3_%d" % (gi % 2)
            self.dma(oT[:, :, :N], oT_s[:, :, g0:g0 + N], writes=["oT"])
            self.dma(R[:, :, :N], xT_f[:, :, g0:g0 + N], writes=[Rk])
            for ob in range(2):
                wt, wk = get_w(("wo", gi, ob))
                w3 = wt[:, :].rearrange("p (k c) -> p k c", c=512)
                for o4 in range(4):
                    oc = ob * 4 + o4
                    bk = st["acc"] % 2
                    st["acc"] += 1
                    for kc in range(NCH):
                        self.mm(ps[bk][:, :N], w3[:, kc, o4 * 128:(o4 + 1) * 128], oT[:, kc, :N],
                                kc == 0, kc == NCH - 1, [wk, "oT"], ["ps%d" % bk])
                    self.stt("dve", R[:, oc, :N], R[:, oc, :N], ALPHA, ps[bk][:, :N], ALU.mult, ALU.add,
                             ["ps%d" % bk, Rk], [Rk])

        def drain(gen, k=None):
            if gen is None:
                return None
            try:
                if k is None:
                    while True:
                        next(gen)
                else:
                    for _ in range(k):
                        next(gen)
            except StopIteration:
                return None
            return gen

        def finish_group(gi):
            g0, N = groups[gi]
            R, Rk = Rb[gi % 2], "R3_%d" % (gi % 2)
            if not last_layer:
                self.dma(xT_f[:, :, g0:g0 + N], R[:, :, :N], reads=[Rk])
                self.dma(xT_b[:, :, g0:g0 + N], vbt[:, :, :N], reads=["vbt"])
            else:
                ntile = max(1, N // 128)
                for ti in range(ntile):
                    nt = min(128, N)
                    c0 = ti * 128
                    yb = ti % 2
                    for half in range(2):
                        bk = 6 + half
                        for cc in range(4):
                            c = half * 4 + cc
                            self.tr(ps[bk][:nt, cc * 128:(cc + 1) * 128], R[:, c, c0:c0 + nt], self.ident_f,
                                    [Rk, "cf"], ["ps%d" % bk])
                        self.cp("dve" if half == 0 else "act", yt[yb][:nt, half * 512:(half + 1) * 512],
                                ps[bk][:nt, :], ["ps%d" % bk], ["yt%d" % yb])
                    if N == DS:
                        self.dma(self.ys[0:nt, :], yt[yb][:nt, :], reads=["yt%d" % yb])
                    else:
                        self.dma(self.yp[g0 + c0:g0 + c0 + nt, :], yt[yb][:nt, :], reads=["yt%d" % yb])

        ln2_prev = None
        wo_part(0)
        drain(self.ln_gen(l, 1, Rb[0], "R3_0", groups[0][1], xbf, "xbf", vbt, sqt, sm))
        for gi, (g0, N) in enumerate(groups):
            R, Rk = Rb[gi % 2], "R3_%d" % (gi % 2)
            for fb in range(8):
                wt, wk = get_w(("wu", gi, fb))
                w3 = wt[:, :].rearrange("p (k c) -> p k c", c=512)
                for f4 in range(4):
                    f = fb * 4 + f4
                    bk = 2 + st["hc"] % 2
                    st["hc"] += 1
                    for kc in range(NCH):
                        self.mm(ps[bk][:, :N], w3[:, kc, f4 * 128:(f4 + 1) * 128], xbf[:, kc, :N],
                                kc == 0, kc == NCH - 1, [wk, "xbf"], ["ps%d" % bk])
                    rb = f % 2
                    self.act(rl[rb][:, :N], ps[bk][:, :N], AF.Relu, ["ps%d" % bk], ["rl%d" % rb])
                    self.tt("pool", hT[:, f, :N], rl[rb][:, :N], rl[rb][:, :N], ALU.mult, ["rl%d" % rb], ["hT%d" % f])
                    if f >= 2:
                        ln2_prev = drain(ln2_prev, 1)
            if gi > 0:
                drain(ln2_prev)
                ln2_prev = None
                finish_group(gi - 1)
            ln1_next = None
            for oc in range(8):
                wt, wk = get_w(("wd", gi, oc))
                w3 = wt[:, :].rearrange("p (f c) -> p f c", c=128)
                bk = st["acc"] % 2
                st["acc"] += 1
                for f in range(32):
                    self.mm(ps[bk][:, :N], w3[:, f, :], hT[:, f, :N], f == 0, f == 31, [wk, "hT%d" % f], ["ps%d" % bk])
                self.stt("dve", R[:, oc, :N], R[:, oc, :N], ALPHA, ps[bk][:, :N], ALU.mult, ALU.add,
                         ["ps%d" % bk, Rk], [Rk])
                if oc == 1 and gi + 1 < NG:
                    wo_part(gi + 1)
                    ln1_next = self.ln_gen(l, 1, Rb[(gi + 1) % 2], "R3_%d" % ((gi + 1) % 2), groups[gi + 1][1],
                                           xbf, "xbf", vbt, sqt, sm)
                elif oc >= 2:
                    ln1_next = drain(ln1_next, 5)
            drain(ln1_next)
            ln2_prev = self.ln_gen(l, 2, R, Rk, N, vbt, "vbt", vbt, sqt, sm)
        drain(ln2_prev)
        finish_group(NG - 1)
        self.S.barrier()

    def build(self):
        self.marks = []
        mk = lambda name: self.marks.append((name, len(self.S.ops["pe"])))
        self.declare()
        self.prologue()
        mk("prologue")
        self.load_wqkv(0)
        sa = getattr(self, "stop_after", None)
        for l in range(self.NL):
            self.phase1(l)
            mk("L%d.qkv" % l)
            if sa == (l, 1):
                break
            self.phase2(l)
            mk("L%d.attn" % l)
            if sa == (l, 2):
                break
            if l + 1 < self.NL:
                self.load_wqkv(l + 1)
            self.phase3(l)
            mk("L%d.mlp" % l)
            if sa == (l, 3):
                break
        self.S.emit(self.nc)
        return self.nc


def make_consts():
    c = np.zeros((128, 7, 128), np.float32)
    p = np.arange(128)[:, None]
    i = np.arange(128)[None, :]
    c[:, 0] = (p == i)
    c[:, 1] = np.where(p >= i, -1.0, 0.0)
    c[:, 2] = -1.0
    c[:, 3] = 1.0
    c[:, 4] = np.where(p < i, 0.0, -10000.0)
    c[:, 5] = np.where((p // 64) <= (i // 64), 0.0, -10000.0)
    c[:, 6] = 1.0 / 128.0
    return c.reshape(128, 7 * 128)


def make_rope(SP, SC):
    pos = np.concatenate([np.arange(SP), SC + np.arange(DS)]).astype(np.float32)
    inv = (np.float32(ROPE_THETA) ** (-np.arange(0, 16, 2, dtype=np.float32) / np.float32(16))).astype(np.float32)
    ang = (pos[:, None] * inv[None, :]).astype(np.float32)
    return np.concatenate([np.cos(ang), np.sin(ang)], axis=1).astype(np.float32)


_CACHE = {}


def run(inputs, SP, SC, NL, n_cores, same_engine_sync=True, debug=False, stop_after=None):
    key = (SP, SC, NL, same_engine_sync, debug, stop_after)
    if key not in _CACHE:
        bld = Builder(SP, SC, NL, same_engine_sync, debug)
        bld.stop_after = stop_after
        _CACHE[key] = bld.build()
    nc = _CACHE[key]
    f = lambda a: np.ascontiguousarray(np.asarray(a, dtype=np.float32))
    consts = make_consts()
    rope = make_rope(SP, SC)
    shared = {
        "sb_wqkv": f(inputs["sb_w_qkv"]), "sb_wo": f(inputs["sb_w_o"]),
        "df_wqkv": f(inputs["diff_w_qkv"]), "df_wo": f(inputs["diff_w_o"]),
        "lq1": f(inputs["diff_lambda_q1"]), "lk1": f(inputs["diff_lambda_k1"]),
        "lq2": f(inputs["diff_lambda_q2"]), "lk2": f(inputs["diff_lambda_k2"]),
        "subg": f(inputs["diff_subln_g"]),
        "ln1g": f(inputs["ln1_g"]), "ln1b": f(inputs["ln1_b"]),
        "ln2g": f(inputs["ln2_g"]), "ln2b": f(inputs["ln2_b"]),
        "wup": f(inputs["mlp_w_up"]), "wdn": f(inputs["mlp_w_down"]),
        "consts": consts, "rope": rope,
    }
    in_maps = []
    for b in range(n_cores):
        m = dict(shared)
        m["xp"] = f(inputs["x_prompt"][b])
        m["xs"] = f(inputs["x_sample"][b])
        m["csbk"] = f(np.asarray(inputs["cache_sb_k"])[:, b]).reshape(2, SC, D)
        m["csbv"] = f(np.asarray(inputs["cache_sb_v"])[:, b]).reshape(2, SC, D)
        m["cdfk"] = f(np.asarray(inputs["cache_diff_k"])[:, b]).reshape(2, SC, D)
        m["cdfv"] = f(np.asarray(inputs["cache_diff_v"])[:, b]).reshape(2, SC, D)
        in_maps.append(m)
    res = run_bass_kernel_spmd(nc, in_maps, core_ids=list(range(n_cores)))
    R = res.results
    if debug:
        return R
    st = lambda k, ax: np.stack([np.asarray(r[k]) for r in R], axis=ax)
    y_p = st("yp", 0)
    y_s = st("ys", 0)
    nsb = 2
    sbk_p = st("o_sbk_p", 1).reshape(nsb, n_cores, SP, 16, 64)
    sbv_p = st("o_sbv_p", 1).reshape(nsb, n_cores, SP, 16, 64)
    dfk_p = st("o_dfk_p", 1).reshape(2, n_cores, SP, 8, 2, 64)
    dfv_p = st("o_dfv_p", 1).reshape(2, n_cores, SP, 8, 128)
    sbk_s = st("o_sbk_s", 1).reshape(nsb, n_cores, DS, 16, 64)
    sbv_s = st("o_sbv_s", 1).reshape(nsb, n_cores, DS, 16, 64)
    dfk_s = st("o_dfk_s", 1).reshape(2, n_cores, DS, 8, 2, 64)
    dfv_s = st("o_dfv_s", 1).reshape(2, n_cores, DS, 8, 128)
    return (y_p, y_s, sbk_p, sbv_p, dfk_p, dfv_p, sbk_s, sbv_s, dfk_s, dfv_s)


def kernel(**inputs):
    return run(inputs, 4096, 2048, 4, 8)
```

```python
import math
import contextlib
import numpy as np
import concourse.bass as bass
import concourse.mybir as mybir
from concourse.bass_utils import run_bass_kernel_spmd

F32 = mybir.dt.float32
BF16 = mybir.dt.bfloat16
AF = mybir.ActivationFunctionType
ALU = mybir.AluOpType

D = 1024
DFF = 4096
DS = 32
NCH = 8
LN_EPS = 1e-5
SUBLN_EPS = 1e-5
DEPTH = 4
ALPHA = (2 * DEPTH) ** 0.25
ROPE_THETA = 500000.0

COMPUTE = ("pe", "act", "dve", "pool")
N_FG = 24
N_BG = 0


class Sched:
    def __init__(self, same_engine_sync=True):
        self.ops = {e: [] for e in ("pe", "act", "dve", "pool", "sp")}
        self.last_w = {}
        self.readers = {}
        self.nfg = 0
        self.nbg = 0
        self.sem_count = [0] * (N_FG + N_BG)
        self.same_engine_sync = same_engine_sync
        self.barrier_deps = []
        self.barrier_id = 0
        self.eng_barrier = {e: 0 for e in self.ops}

    def barrier(self):
        deps = []
        for e in COMPUTE:
            if self.ops[e]:
                for i in range(len(self.ops[e]) - 1, -1, -1):
                    if self.ops[e][i]["me"][0] == "c":
                        deps.append(("c", e, i))
                        break
        for s in range(N_FG):
            if self.sem_count[s] > 0:
                deps.append(("d", s, self.sem_count[s]))
        self.barrier_deps = deps
        self.barrier_id += 1

    def _add(self, eng, fn, reads, writes, dma=None, bg=False):
        deps = []
        writes = list(writes) + [k for k in reads if k.startswith("ps")]
        reads = [k for k in reads if not k.startswith("ps")]
        for k in reads:
            w = self.last_w.get(k)
            if w is not None:
                deps.append(w)
        for k in writes:
            w = self.last_w.get(k)
            if w is not None:
                deps.append(w)
            deps.extend(self.readers.get(k, ()))
        if self.eng_barrier[eng] != self.barrier_id:
            deps.extend(self.barrier_deps)
            self.eng_barrier[eng] = self.barrier_id
        idx = len(self.ops[eng])
        if dma is None:
            me = ("c", eng, idx)
        else:
            if bg:
                s = N_FG + self.nbg % N_BG
                self.nbg += 1
            else:
                s = self.nfg % N_FG
                self.nfg += 1
            prev = self.sem_count[s]
            if prev > 0:
                deps.append(("d", s, prev))
            self.sem_count[s] = prev + 1
            me = ("d", s, prev + 1)
        fdeps = []
        for d in deps:
            if d[0] == "c" and d[1] == eng:
                if eng == "pe" or not self.same_engine_sync:
                    continue
                if d[2] == idx:
                    continue
            fdeps.append(d)
        self.ops[eng].append({"fn": fn, "deps": fdeps, "me": me, "sig": False})
        for k in reads:
            self.readers.setdefault(k, []).append(me)
        for k in writes:
            self.last_w[k] = me
            self.readers[k] = []
        return me

    def op(self, eng, fn, reads=(), writes=()):
        return self._add(eng, fn, reads, writes)

    def dma(self, q, out, in_, reads=(), writes=(), bg=False, **kw):
        return self._add(q, (out, in_, kw), reads, writes, dma=True, bg=bg)

    def emit(self, nc):
        for e, lst in self.ops.items():
            for o in lst:
                for d in o["deps"]:
                    if d[0] == "c":
                        self.ops[d[1]][d[2]]["sig"] = True
        rank = {}
        for e, lst in self.ops.items():
            c = 0
            for i, o in enumerate(lst):
                if o["sig"]:
                    c += 1
                    rank[(e, i)] = c
        with contextlib.ExitStack() as st:
            csem = {e: st.enter_context(nc.semaphore("cs_" + e)) for e in COMPUTE}
            dsem = [st.enter_context(nc.semaphore("ds_%d" % i)) for i in range(N_FG + N_BG)]
            block = st.enter_context(nc.Block())

            def run(eng_name):
                def body(eng):
                    seen = {}
                    for i, o in enumerate(self.ops[eng_name]):
                        need = {}
                        for d in o["deps"]:
                            if d[0] == "c":
                                key = ("c", d[1])
                                val = rank[(d[1], d[2])]
                            else:
                                key = ("d", d[1])
                                val = 16 * d[2]
                            if val > seen.get(key, 0):
                                need[key] = max(need.get(key, 0), val)
                        for key, val in need.items():
                            sem = csem[key[1]] if key[0] == "c" else dsem[key[1]]
                            eng.wait_ge(sem, val)
                            seen[key] = val
                        me = o["me"]
                        if me[0] == "d":
                            out, in_, kw = o["fn"]
                            eng.dma_start(out=out, in_=in_, **kw).then_inc(dsem[me[1]], 16)
                        else:
                            ins = o["fn"](eng)
                            if o["sig"]:
                                ins.then_inc(csem[eng_name], 1)
                    if eng_name == "sp":
                        for s in range(N_FG + N_BG):
                            if self.sem_count[s] > 0:
                                eng.wait_ge(dsem[s], 16 * self.sem_count[s])
                return body

            block.sync(run("sp"))
            block.tensor(run("pe"))
            block.scalar(run("act"))
            block.vector(run("dve"))
            block.gpsimd(run("pool"))


class Alloc:
    def __init__(self, nc, base, limit=229344):
        self.nc, self.off, self.limit = nc, base, limit
        self.n = 0

    def t(self, name, shape, dt):
        sz = 2 if dt == BF16 else 4
        nb = sz
        for s in shape[1:]:
            nb *= s
        nb = (nb + 63) // 64 * 64
        Alloc.cnt = getattr(Alloc, "cnt", 0) + 1
        h = self.nc.alloc_sbuf_tensor_at("%s_%d" % (name, Alloc.cnt), list(shape), dt, offset=self.off)
        self.off += nb
        assert self.off <= self.limit, ("SBUF overflow", name, self.off)
        return h


class Builder:
    def __init__(self, SP, SC, NL, same_engine_sync=True, debug=False):
        self.debug = debug
        self.SP, self.SC, self.NL = SP, SC, NL
        self.T = SP + DS
        self.nc = bass.Bass("TRN2", target_bir_lowering=False)
        self.S = Sched(same_engine_sync)
        self.uid = 0

    def mm(self, out, lhsT, rhs, start, stop, reads, writes, tp=None):
        kw = {}
        if tp is not None:
            kw["tile_position"] = tp
        self.S.op("pe", lambda e: e.matmul(out, lhsT=lhsT, rhs=rhs, start=start, stop=stop,
                                            skip_group_check=True, **kw), reads, writes)

    def tr(self, out, in_, ident, reads, writes):
        self.S.op("pe", lambda e: e.transpose(out=out, in_=in_, identity=ident), reads, writes)

    def act(self, out, in_, func, reads, writes, scale=1.0, bias=0.0):
        self.S.op("act", lambda e: e.activation(out=out, in_=in_, func=func, bias=bias, scale=scale),
                  reads, writes)

    def cp(self, eng, out, in_, reads, writes):
        if eng == "act":
            self.S.op("act", lambda e: e.copy(out=out, in_=in_), reads, writes)
        else:
            self.S.op(eng, lambda e: e.tensor_copy(out=out, in_=in_), reads, writes)

    def tt(self, eng, out, in0, in1, op, reads, writes):
        self.S.op(eng, lambda e: e.tensor_tensor(out=out, in0=in0, in1=in1, op=op), reads, writes)

    def ts(self, eng, out, in0, s1, s2, op0, op1, reads, writes):
        if s2 is None:
            self.S.op(eng, lambda e: e.tensor_scalar(out=out, in0=in0, scalar1=s1, scalar2=None, op0=op0),
                      reads, writes)
        else:
            self.S.op(eng, lambda e: e.tensor_scalar(out=out, in0=in0, scalar1=s1, scalar2=s2, op0=op0, op1=op1),
                      reads, writes)

    def stt(self, eng, out, in0, scalar, in1, op0, op1, reads, writes):
        self.S.op(eng, lambda e: e.scalar_tensor_tensor(out=out, in0=in0, scalar=scalar, in1=in1, op0=op0, op1=op1),
                  reads, writes)

    def memset(self, eng, ap, val, writes):
        self.S.op(eng, lambda e: e.memset(ap, val), (), writes)

    def recip(self, out, in_, reads, writes):
        self.S.op("dve", lambda e: e.reciprocal(out=out, in_=in_), reads, writes)

    def dma(self, out, in_, reads=(), writes=(), q="sp", **kw):
        self.S.dma(q, out, in_, reads, writes, **kw)

    def declare(self):
        nc, SP, SC, T = self.nc, self.SP, self.SC, self.T
        I = lambda n, s: nc.dram_tensor(n, list(s), F32, kind="ExternalInput").ap()
        O = lambda n, s: nc.dram_tensor(n, list(s), F32, kind="ExternalOutput").ap()
        Sc = lambda n, s, dt: nc.dram_tensor(n, list(s), dt, kind=("ExternalOutput" if self.debug else "Internal")).ap()
        self.xp = I("xp", [SP, D]); self.xs = I("xs", [DS, D])
        self.csbk = I("csbk", [2, SC, D]); self.csbv = I("csbv", [2, SC, D])
        self.cdfk = I("cdfk", [2, SC, D]); self.cdfv = I("cdfv", [2, SC, D])
        self.sb_wqkv = I("sb_wqkv", [2, D, 3 * D]); self.sb_wo = I("sb_wo", [2, D, D])
        self.df_wqkv = I("df_wqkv", [2, D, 3 * D]); self.df_wo = I("df_wo", [2, D, D])
        self.lq1 = I("lq1", [2, 64]); self.lk1 = I("lk1", [2, 64])
        self.lq2 = I("lq2", [2, 64]); self.lk2 = I("lk2", [2, 64])
        self.subg = I("subg", [2, 128])
        self.ln1g = I("ln1g", [4, D]); self.ln1b = I("ln1b", [4, D])
        self.ln2g = I("ln2g", [4, D]); self.ln2b = I("ln2b", [4, D])
        self.wup = I("wup", [4, D, DFF]); self.wdn = I("wdn", [4, DFF, D])
        self.consts = I("consts", [128, 7 * 128])
        self.rope = I("rope", [T, 16])
        self.yp = O("yp", [SP, D]); self.ys = O("ys", [DS, D])
        self.o_sbk_p = O("o_sbk_p", [2, SP, D]); self.o_sbv_p = O("o_sbv_p", [2, SP, D])
        self.o_dfk_p = O("o_dfk_p", [2, SP, D]); self.o_dfv_p = O("o_dfv_p", [2, SP, D])
        self.o_sbk_s = O("o_sbk_s", [2, DS, D]); self.o_sbv_s = O("o_sbv_s", [2, DS, D])
        self.o_dfk_s = O("o_dfk_s", [2, DS, D]); self.o_dfv_s = O("o_dfv_s", [2, DS, D])
        self.xT_f = Sc("xT_f", [NCH, 128, T], F32)
        self.xT_b = Sc("xT_b", [NCH, 128, T], BF16)
        self.qT_s = Sc("qT_s", [NCH, 128, T], BF16)
        self.kT_p = Sc("kT_p", [NCH, 128, SP], BF16)
        self.kT_c = Sc("kT_c", [NCH, 128, SC + DS], BF16)
        self.v_p = Sc("v_p", [SP, D], BF16)
        self.v_c = Sc("v_c", [SC + DS, D], BF16)
        self.oT_s = Sc("oT_s", [NCH, 128, T], BF16)
        NL = self.NL
        self.wqkv_b = Sc("wqkv_b", [NL, 128, NCH, 3 * D], BF16)
        self.wo_b = Sc("wo_b", [NL, 2, 128, NCH, 512], BF16)
        self.wup_b = Sc("wup_b", [NL, 8, 128, NCH, 512], BF16)
        self.wdn_b = Sc("wdn_b", [NL, 8, 128, 32, 128], BF16)

    def cast_steps(self, l, only=None, engs=("pool",)):
        j = l // 2
        wq = (self.sb_wqkv if l % 2 == 0 else self.df_wqkv)[j]
        wo = (self.sb_wo if l % 2 == 0 else self.df_wo)[j].rearrange("(kc p) (ob c) -> ob p kc c", p=128, c=512)
        wu = self.wup[l].rearrange("(kc p) (fb c) -> fb p kc c", p=128, c=512)
        wd = self.wdn[l].rearrange("(f p) (oc c) -> oc p f c", p=128, c=128)
        steps = []

        def mk(kind, i):
            def go():
                b = self.stg_n % 2
                self.stg_n += 1
                sf, sb_ = self.stg_f[b], self.stg_b[b]
                fk, bk = "stgf%d" % b, "stgb%d" % b
                n = 4096
                if kind == "wq":
                    self.dma(sf[:, 0:3 * D], wq[i * 128:(i + 1) * 128, :], writes=[fk])
                    n = 3 * D
                    dst = self.wqkv_b[l][:, i, :]
                elif kind == "wo":
                    self.dma(sf[:, :].rearrange("p (k c) -> p k c", c=512), wo[i], writes=[fk])
                    dst = self.wo_b[l, i].rearrange("p k c -> p (k c)")
                elif kind == "wu":
                    self.dma(sf[:, :].rearrange("p (k c) -> p k c", c=512), wu[i], writes=[fk])
                    dst = self.wup_b[l, i].rearrange("p k c -> p (k c)")
                else:
                    s3 = sf[:, :].rearrange("p (f c) -> p f c", c=128)
                    for q4 in range(4):
                        self.dma(s3[:, q4 * 8:(q4 + 1) * 8, :], wd[i][:, q4 * 8:(q4 + 1) * 8, :], writes=[fk + "_%d" % q4])
                    dst = self.wdn_b[l, i].rearrange("p f c -> p (f c)")
                rk = [fk] if kind != "wd" else [fk + "_%d" % q for q in range(4)]
                self.cp(engs[self.stg_n % len(engs)], sb_[:, 0:n], sf[:, 0:n], rk, [bk] + rk)

                def store():
                    self.dma(dst, sb_[:, 0:n], reads=[bk], writes=["%s%d_%d" % (kind, l, i)])
                return store
            return go
        for kind, cnt in (("wq", 8), ("wo", 2), ("wu", 8), ("wd", 8)):
            if only is not None and kind not in only:
                continue
            for i in range(cnt):
                steps.append(mk(kind, i))
        return steps

    def alloc_staging(self, A):
        self.stg_f = [A.t("stgf", [128, 4096], F32) for _ in range(2)]
        self.stg_b = [A.t("stgb", [128, 4096], BF16) for _ in range(2)]
        self.stg_n = 0

    def prologue(self):
        nc, S = self.nc, self.S
        A = Alloc(nc, 16512)
        self.cf = A.t("cf", [128, 7 * 128], F32)
        self.cb = A.t("cb", [128, 7 * 128], BF16)
        cb = self.cb
        self.ident_b = cb[:, 0:128]
        self.negtri = cb[:, 128:256]
        self.negones = cb[:, 256:384]
        self.ones_b = cb[:, 384:512]
        self.mask_sb = cb[:, 512:640]
        self.mask_df = cb[:, 640:768]
        self.onesm = cb[:, 768:896]
        self.ident_f = self.cf[:, 0:128]
        self.lnp = A.t("lnp", [128, 4, 4, NCH], F32)
        self.lam = A.t("lam", [128, 2, 8], F32)
        self.lamv = A.t("lamv", [128, 2, 4, 64], F32)
        self.ones1024 = A.t("ones1024", [128, 128], BF16)
        self.wq_sb = A.t("wq_sb", [128, NCH, 3 * D], BF16)
        self.persist_end = A.off
        self.pp = [nc.alloc_psum_tensor("psp%d" % i, [128, 2, 512], F32) for i in range(4)]
        self.ps = [self.pp[i // 2][:, i % 2, :] for i in range(8)]

        self.dma(self.cf[:], self.consts, writes=["cf"])
        self.cp("dve", cb[:], self.cf[:], ["cf"], ["cb"])
        self.ts("dve", self.ones1024[:], self.cf[:, 384:512], 1.0 / 1024.0, None, ALU.mult, None, ["cf"], ["cb"])
        for l in range(self.NL):
            for i, src in enumerate((self.ln1g, self.ln1b, self.ln2g, self.ln2b)):
                self.dma(self.lnp[:, l, i, :], src[l].rearrange("(c p) -> p c", p=128), writes=["lnp"],
                         allow_slow_non_contiguous=True)
        for j in range(self.NL // 2):
            l = 2 * j + 1
            lam_init = 0.8 - 0.6 * math.exp(-0.3 * l)
            for i, src in enumerate((self.lq1, self.lk1, self.lq2, self.lk2)):
                self.dma(self.lamv[:, j, i:i + 1, :], src[j:j + 1, :].partition_broadcast(128), writes=["lamv"])
            self.dma(self.lam[:, j, 4:5], self.subg[j].rearrange("(p o) -> p o", o=1), writes=["lamg"],
                     allow_slow_non_contiguous=True)
            lv = self.lamv
            self.tt("dve", lv[:, j, 0, :], lv[:, j, 0, :], lv[:, j, 1, :], ALU.mult, ["lamv"], ["lamv"])
            self.tt("dve", lv[:, j, 2, :], lv[:, j, 2, :], lv[:, j, 3, :], ALU.mult, ["lamv"], ["lamv"])
            self.S.op("dve", (lambda jj: lambda e: e.reduce_sum(out=self.lam[:, jj, 0:1], in_=lv[:, jj, 0, :],
                                                               axis=mybir.AxisListType.X))(j), ["lamv"], ["lam"])
            self.S.op("dve", (lambda jj: lambda e: e.reduce_sum(out=self.lam[:, jj, 1:2], in_=lv[:, jj, 2, :],
                                                               axis=mybir.AxisListType.X))(j), ["lamv"], ["lam"])
            self.act(self.lam[:, j, 0:2], self.lam[:, j, 0:2], AF.Exp, ["lam"], ["lam"])
            self.tt("dve", self.lam[:, j, 2:3], self.lam[:, j, 1:2], self.lam[:, j, 0:1], ALU.subtract, ["lam"], ["lam"])
            self.ts("dve", self.lam[:, j, 3:4], self.lam[:, j, 2:3], -lam_init, None, ALU.add, None, ["lam"], ["lam"])
            self.ts("dve", self.lam[:, j, 5:6], self.lam[:, j, 4:5], 1.0 - lam_init, None, ALU.mult, None,
                    ["lamg", "lam"], ["lam"])
        A2 = Alloc(nc, self.persist_end)
        self.alloc_staging(A2)
        for st_ in self.cast_steps(0, only=("wq",), engs=("dve", "act")):
            st_()()
        self.pending_casts = self.cast_steps(0, only=("wo", "wu", "wd"))
        for l in range(1, self.NL):
            self.pending_casts += self.cast_steps(l)
        xin = [A2.t("xin", [128, D], F32) for _ in range(2)]
        xtf = [A2.t("xtf", [128, NCH, 128], F32) for _ in range(2)]
        xtb = [A2.t("xtb", [128, NCH, 128], BF16) for _ in range(2)]
        tiles = [(self.xp, i * 128, 128, i * 128) for i in range(self.SP // 128)] + [(self.xs, 0, DS, self.SP)]
        xT_f = self.xT_f.rearrange("c p t -> p c t")
        xT_b = self.xT_b.rearrange("c p t -> p c t")
        pend_st = []
        for n, (src, r0, nt, t0) in enumerate(tiles):
            b = n % 2
            self.dma(xin[b][:nt, :], src[r0:r0 + nt, :], writes=["xin%d" % b])
            for half in range(2):
                bank = self.ps[(2 * n + half) % 8]
                key = "ps%d" % ((2 * n + half) % 8)
                for cc in range(4):
                    c = half * 4 + cc
                    self.tr(bank[:, cc * 128:cc * 128 + nt], xin[b][:nt, c * 128:(c + 1) * 128],
                            self.ident_f[:nt, :nt], ["xin%d" % b, "cf"], [key])
                src_v = bank[:].rearrange("p (c t) -> p c t", t=128)[:, :, 0:nt]
                self.cp("dve", xtf[b][:, half * 4:half * 4 + 4, 0:nt], src_v, [key], ["xtf%d" % b])
                self.cp("act", xtb[b][:, half * 4:half * 4 + 4, 0:nt], src_v, [key], ["xtb%d" % b])
            for (o_, i_, rk_) in pend_st:
                self.dma(o_, i_, reads=rk_)
            pend_st = [(xT_f[:, :, t0:t0 + nt], xtf[b][:, :, 0:nt], ["xtf%d" % b]),
                       (xT_b[:, :, t0:t0 + nt], xtb[b][:, :, 0:nt], ["xtb%d" % b])]
        for (o_, i_, rk_) in pend_st:
            self.dma(o_, i_, reads=rk_)
        self.S.barrier()

    def load_wqkv(self, l):
        for kc in range(NCH):
            self.dma(self.wq_sb[:, kc, :], self.wqkv_b[l][:, kc, :], reads=["wq%d_%d" % (l, kc)], writes=["wq_sb"])

    def phase1(self, l):
        nc, SP, SC = self.nc, self.SP, self.SC
        diff = (l % 2 == 1)
        j = l // 2
        A = Alloc(nc, self.persist_end)
        xb = [A.t("xb", [128, NCH, 512], BF16) for _ in range(2)]
        qTg = [A.t("qTg", [128, NCH, 512], BF16) for _ in range(2)]
        kTg = [A.t("kTg", [128, NCH, 512], BF16) for _ in range(2)]
        kvf = [A.t("kvf", [128, 2 * D], F32) for _ in range(2)]
        qf = [A.t("qf", [128, D], F32) for _ in range(2)] if diff else None
        qb = [A.t("qb", [128, D], BF16) for _ in range(2)]
        kb = [A.t("kb", [128, D], BF16) for _ in range(2)]
        vb = [A.t("vb", [128, D], BF16) for _ in range(2)]
        if diff:
            rt = [A.t("rt", [128, 16], F32) for _ in range(2)]
            rtmp = [A.t("rtmp", [128, 4, 16, 8], F32) for _ in range(2)]
        kcin = [A.t("kcin", [128, D], BF16) for _ in range(2)]
        kcT = [A.t("kcT", [128, NCH, 128], BF16) for _ in range(2)]
        xT_b = self.xT_b.rearrange("c p t -> p c t")
        qT_s = self.qT_s.rearrange("c p t -> p c t")
        kT_p = self.kT_p.rearrange("c p t -> p c t")
        kT_c = self.kT_c.rearrange("c p t -> p c t")
        if diff:
            Kp, Vp, Ks, Vs = self.o_dfk_p[j], self.o_dfv_p[j], self.o_dfk_s[j], self.o_dfv_s[j]
            ck = self.cdfk[j]
        else:
            Kp, Vp, Ks, Vs = self.o_sbk_p[j], self.o_sbv_p[j], self.o_sbk_s[j], self.o_sbv_s[j]
            ck = self.csbk[j]
        cstg = [A.t("cstg", [128, D], F32) for _ in range(2)]
        cvb = [A.t("cvb", [128, D], BF16) for _ in range(2)]
        cv = (self.cdfv if diff else self.csbv)[j]

        pend = {"cur": [], "prev": []}

        def defer(out, in_, rkeys):
            pend["cur"].append((out, in_, rkeys))

        def rotate():
            for (o_, i_, rk_) in pend["prev"]:
                self.dma(o_, i_, reads=rk_)
            pend["prev"] = pend["cur"]
            pend["cur"] = []

        def cache_tile(ci):
            b = ci % 2
            self.dma(cstg[0][:, :], ck[ci * 128:(ci + 1) * 128, :], writes=["cstg0"])
            self.cp("dve", kcin[b][:, :], cstg[0][:, :], ["cstg0"], ["kcin%d" % b])
            self.dma(cstg[1][:, :], cv[ci * 128:(ci + 1) * 128, :], writes=["cstg1"])
            self.cp("act", cvb[b][:, :], cstg[1][:, :], ["cstg1"], ["cvb%d" % b])
            defer(self.v_c[ci * 128:(ci + 1) * 128, :], cvb[b][:, :], ["cvb%d" % b])
            bk = 4 + b
            bank, key = self.ps[bk], "ps%d" % bk
            bview = bank[:].bitcast(BF16)
            for c in range(NCH):
                self.tr(bview[:, c * 128:(c + 1) * 128], kcin[b][:, c * 128:(c + 1) * 128], self.ident_b,
                        ["kcin%d" % b, "cb"], [key])
            self.cp("dve" if b == 0 else "act", kcT[b][:, :, :], bview.rearrange("p (c t) -> p c t", t=128),
                    [key], ["kcT%d" % b])
            defer(kT_c[:, :, ci * 128:(ci + 1) * 128], kcT[b][:, :, :], ["kcT%d" % b])

        groups = [(g * 512, 512, False) for g in range(SP // 512)] + [(SP, DS, True)]
        cache_done = 0
        tcnt = 0
        bankc = 0
        for gi, (g0, NT, is_s) in enumerate(groups):
            gb = gi % 2
            self.dma(xb[gb][:, :, 0:NT], xT_b[:, :, g0:g0 + NT], writes=["xb%d" % gb])
            ntile = max(1, NT // 128)
            for ti in range(ntile):
                nt = min(128, NT)
                c0 = ti * 128
                tb = tcnt % 2
                tcnt += 1
                tok = g0 + c0
                for nb in range(6):
                    bk = bankc % 4
                    bankc += 1
                    bank, key = self.ps[bk], "ps%d" % bk
                    for kc in range(NCH):
                        self.mm(bank[:nt, :], xb[gb][:, kc, c0:c0 + nt], self.wq_sb[:, kc, nb * 512:(nb + 1) * 512],
                                kc == 0, kc == NCH - 1, ["xb%d" % gb, "wq_sb"], [key])
                    if nb < 2:
                        if diff:
                            self.cp("act", qf[tb][:nt, nb * 512:(nb + 1) * 512], bank[:nt, :], [key], ["qf%d" % tb])
                        else:
                            self.act(qb[tb][:nt, nb * 512:(nb + 1) * 512], bank[:nt, :], AF.Copy, [key],
                                     ["qb%d" % tb], scale=0.125)
                    else:
                        eng = "dve" if nb % 2 == 0 else "act"
                        self.cp(eng, kvf[tb][:nt, (nb - 2) * 512:(nb - 1) * 512], bank[:nt, :], [key],
                                ["kvf%d_%d" % (tb, (nb - 2) // 2)])
                kkey, vkey = "kvf%d_0" % tb, "kvf%d_1" % tb
                if diff:
                    self.dma(rt[tb][:nt, :], self.rope[tok:tok + nt, :], writes=["rt%d" % tb])
                    cosb = rt[tb][:nt, 0:8].unsqueeze(1).to_broadcast([nt, 16, 8])
                    sinb = rt[tb][:nt, 8:16].unsqueeze(1).to_broadcast([nt, 16, 8])
                    for (buf, bkey) in ((qf[tb][:nt, :], "qf%d" % tb), (kvf[tb][:nt, 0:D], kkey)):
                        v3 = buf.rearrange("p (g d) -> p g d", d=64)
                        x1, x2 = v3[:, :, 0:8], v3[:, :, 8:16]
                        tmp = rtmp[tb]
                        rk = "rtmp%d" % tb
                        self.tt("pool", tmp[:nt, 0], x1, cosb, ALU.mult, [bkey, "rt%d" % tb], [rk])
                        self.tt("pool", tmp[:nt, 1], x2, sinb, ALU.mult, [bkey, "rt%d" % tb], [rk])
                        self.tt("pool", tmp[:nt, 2], x2, cosb, ALU.mult, [bkey, "rt%d" % tb], [rk])
                        self.tt("pool", tmp[:nt, 3], x1, sinb, ALU.mult, [bkey, "rt%d" % tb], [rk])
                        self.tt("pool", x1, tmp[:nt, 0], tmp[:nt, 1], ALU.subtract, [rk], [bkey])
                        self.tt("pool", x2, tmp[:nt, 2], tmp[:nt, 3], ALU.add, [rk], [bkey])
                    self.act(qb[tb][:nt, :], qf[tb][:nt, :], AF.Copy, ["qf%d" % tb], ["qb%d" % tb], scale=0.125)
                if is_s:
                    defer(Ks[0:nt, :], kvf[tb][:nt, 0:D], [kkey])
                    defer(Vs[0:nt, :], kvf[tb][:nt, D:2 * D], [vkey])
                else:
                    defer(Kp[tok:tok + nt, :], kvf[tb][:nt, 0:D], [kkey])
                    defer(Vp[tok:tok + nt, :], kvf[tb][:nt, D:2 * D], [vkey])
                self.cp("dve", kb[tb][:nt, :], kvf[tb][:nt, 0:D], [kkey], ["kb%d" % tb])
                self.cp("act", vb[tb][:nt, :], kvf[tb][:nt, D:2 * D], [vkey], ["vb%d" % tb])
                if is_s:
                    defer(self.v_c[SC:SC + nt, :], vb[tb][:nt, :], ["vb%d" % tb])
                else:
                    defer(self.v_p[tok:tok + nt, :], vb[tb][:nt, :], ["vb%d" % tb])
                for (src, skey, dst, dkey, bk) in ((qb[tb], "qb%d" % tb, qTg[gb], "qTg%d" % gb, 4 + tb),
                                                   (kb[tb], "kb%d" % tb, kTg[gb], "kTg%d" % gb, 6 + tb)):
                    bank, key = self.ps[bk], "ps%d" % bk
                    bview = bank[:].bitcast(BF16)
                    for c in range(NCH):
                        self.tr(bview[:, c * 128:c * 128 + nt], src[:nt, c * 128:(c + 1) * 128],
                                self.ident_b[:nt, :nt], [skey, "cb"], [key])
                    eng = "dve" if bk < 6 else "act"
                    self.cp(eng, dst[:, :, c0:c0 + nt], bview.rearrange("p (c t) -> p c t", t=128)[:, :, 0:nt],
                            [key], [dkey])
                if cache_done < SC // 128:
                    cache_tile(cache_done)
                    cache_done += 1
                rotate()
            self.dma(qT_s[:, :, g0:g0 + NT], qTg[gb][:, :, 0:NT], reads=["qTg%d" % gb])
            if is_s:
                self.dma(kT_c[:, :, SC:SC + NT], kTg[gb][:, :, 0:NT], reads=["kTg%d" % gb])
            else:
                self.dma(kT_p[:, :, g0:g0 + NT], kTg[gb][:, :, 0:NT], reads=["kTg%d" % gb])
        while cache_done < SC // 128:
            cache_tile(cache_done)
            cache_done += 1
            rotate()
        rotate()
        rotate()
        self.S.barrier()

    def phase2(self, l):
        nc, SP, SC = self.nc, self.SP, self.SC
        diff = (l % 2 == 1)
        j = l // 2
        NKT = SP // 128
        NCT = SC // 128
        A = Alloc(nc, self.persist_end)
        kT = [A.t("kT", [128, SP], BF16) for _ in range(2)]
        qT = [A.t("qT", [128, SP], BF16) for _ in range(2)]
        vv = [A.t("vv", [128, NKT, 128], BF16) for _ in range(2)]
        kTc = [A.t("kTc", [128, SC + DS], BF16) for _ in range(2)]
        qTc = [A.t("qTc", [128, DS], BF16) for _ in range(2)]
        vc = [A.t("vc", [128, NCT + 1, 128], BF16) for _ in range(2)]
        W = [A.t("W", [128, 2, 512], BF16) for _ in range(3)]
        oTt = [A.t("oTt", [128, 512], BF16) for _ in range(2)]
        if diff:
            rec = A.t("rec", [128, 512], F32)
            rec2 = A.t("rec2", [128, 512], F32)
            recp = A.t("recp", [128, 2, 512], F32)
            o0 = A.t("o0", [128, 512], F32)
            o1 = A.t("o1", [128, 512], F32)
            osq = A.t("osq", [128, 512], BF16)
            lnv = A.t("lnv", [128, 512], F32)
        else:
            E = [A.t("E", [128, 2, 512], F32) for _ in range(2)]
            SPt = [A.t("SPt", [128, 2, 512], BF16) for _ in range(2)]
            Tt = [A.t("Tt", [128, 2, 512], F32) for _ in range(2)]
            R = [A.t("R", [128, 2, 512], F32) for _ in range(2)]
        qT_s, kT_p, kT_c, oT_s = self.qT_s, self.kT_p, self.kT_c, self.oT_s
        ps, pp = self.ps, self.pp
        mask = self.mask_df if diff else self.mask_sb

        def load_chunk(c):
            b = c % 2
            self.dma(kT[b][:, :], kT_p[c], writes=["kT%d" % b])
            self.dma(qT[b][:, :], qT_s[c][:, 0:SP], writes=["qT%d" % b])
            self.dma(vv[b][:, :, :], self.v_p[:, c * 128:(c + 1) * 128].rearrange("(t p) e -> p t e", p=128),
                     writes=["vv%d" % b])
            self.dma(kTc[b][:, :], kT_c[c], writes=["kTc%d" % b])
            self.dma(qTc[b][:, :], qT_s[c][:, SP:SP + DS], writes=["qTc%d" % b])
            self.dma(vc[b][:, 0:NCT, :], self.v_c[0:SC, c * 128:(c + 1) * 128].rearrange("(t p) e -> p t e", p=128),
                     writes=["vc_sb%d" % b])
            self.dma(vc[b][0:DS, NCT, :], self.v_c[SC:SC + DS, c * 128:(c + 1) * 128],
                     writes=["vc_sb%d" % b])

        tasks = []
        blocks = [(B * 512, 512, False) for B in range(SP // 512)] + [(0, DS, True)]
        for c in range(NCH):
            for bi, (q0, nq, is_s) in enumerate(blocks):
                tl = []
                if is_s:
                    tl.append(dict(kind="part", nk=DS, kt=NCT, col0=0, N=DS, masked=(not diff)))
                    for jt in range(NCT - 1, -1, -1):
                        tl.append(dict(kind="full", nk=128, kt=jt, col0=0, N=DS, masked=False))
                else:
                    B = q0 // 512
                    for jt in range(4 * B + 3, -1, -1):
                        if jt >= 4 * B:
                            i = jt - 4 * B
                            tl.append(dict(kind="diag", nk=128, kt=jt, col0=128 * i, N=512 - 128 * i, masked=True))
                        else:
                            tl.append(dict(kind="full", nk=128, kt=jt, col0=0, N=512, masked=False))
                for n, t in enumerate(tl):
                    t.update(c=c, bi=bi, q0=q0, nq=nq, is_s=is_s, first=(n == 0), last=(n == len(tl) - 1))
                    tasks.append(t)
        for n, t in enumerate(tasks):
            t["n"] = n

        def kq_pair(t):
            n = t["n"]
            zp, key = pp[n % 2], "psz%d" % (n % 2)
            b = t["c"] % 2
            nk, col0, N = t["nk"], t["col0"], t["N"]
            mw = min(128, N)
            if t["masked"]:
                for hp in range(2):
                    self.mm(zp[:nk, hp, col0:col0 + mw], self.ident_b[:nk, :nk], mask[:nk, :mw], True, False,
                            ["cb"], [key])
            for hp in range(2):
                pr = slice(hp * 64, hp * 64 + 64)
                if t["is_s"]:
                    kap = kTc[b][pr, t["kt"] * 128:t["kt"] * 128 + nk]
                    qap = qTc[b][pr, :]
                    rk = ["kTc%d" % b, "qTc%d" % b]
                else:
                    kap = kT[b][pr, t["kt"] * 128:t["kt"] * 128 + nk]
                    qap = qT[b][pr, t["q0"]:t["q0"] + 512]
                    rk = ["kT%d" % b, "qT%d" % b]
                self.mm(zp[:nk, hp, col0:col0 + N], kap, qap[:, col0:col0 + N], not t["masked"], True, rk, [key])

        def vap(t, lo, hi):
            b = t["c"] % 2
            if t["is_s"]:
                return vc[b][:t["nk"], t["kt"], lo:hi], "vc_sb%d" % b
            return vv[b][:t["nk"], t["kt"], lo:hi], "vv%d" % b

        def obank(t):
            return (t["c"] * len(blocks) + t["bi"]) % 2

        def sb_A(t):
            kq_pair(t)

        def sb_B(t):
            n, nk, col0, N = t["n"], t["nk"], t["col0"], t["N"]
            zp, key = pp[n % 2], "psz%d" % (n % 2)
            self.act(E[n % 2][:nk, :, :N], zp[:nk, :, col0:col0 + N], AF.Exp, [key], ["E%d" % (n % 2)])
            self.act(SPt[n % 2][:nk, :, :N], E[n % 2][:nk, :, :N], AF.Ln, ["E%d" % (n % 2)], ["SPt%d" % (n % 2)],
                     bias=1.0)

        def sb_C(t):
            n, nk, col0, N = t["n"], t["nk"], t["col0"], t["N"]
            zp, key = pp[n % 2], "psz%d" % (n % 2)
            for hp in range(2):
                self.mm(zp[:nk, hp, col0:col0 + N], self.negtri[:nk, :nk], SPt[n % 2][:nk, hp, :N], False, True,
                        ["cb", "SPt%d" % (n % 2)], [key])
            if not t["last"]:
                for hp in range(2):
                    self.mm(pp[2][:, hp, col0:col0 + N], self.negones[:nk, :], SPt[n % 2][:nk, hp, :N], True, True,
                            ["cb", "SPt%d" % (n % 2)], ["psc"])

        def sb_D(t):
            n, nk, col0, N = t["n"], t["nk"], t["col0"], t["N"]
            zp, key = pp[n % 2], "psz%d" % (n % 2)
            rb = obank(t)
            if t["first"]:
                self.memset("dve", R[rb][:, :, :], 0.0, ["R%d" % rb])
            self.tt("dve", Tt[n % 2][:nk, :, :N], zp[:nk, :, col0:col0 + N], R[rb][:nk, :, col0:col0 + N], ALU.add,
                    [key, "R%d" % rb], ["Tt%d" % (n % 2)])
            if not t["last"]:
                self.tt("dve", R[rb][:, :, col0:col0 + N], R[rb][:, :, col0:col0 + N], pp[2][:, :, col0:col0 + N],
                        ALU.add, ["psc", "R%d" % rb], ["R%d" % rb])

        def sb_E(t):
            n, nk, N = t["n"], t["nk"], t["N"]
            self.act(W[n % 3][:nk, :, :N], Tt[n % 2][:nk, :, :N], AF.Exp, ["Tt%d" % (n % 2)], ["W%d" % (n % 3)])

        def sb_F(t):
            n = t["n"]
            ob = 6 + obank(t)
            col0, N, nk = t["col0"], t["N"], t["nk"]
            for hp in range(2):
                va, vkey = vap(t, hp * 64, hp * 64 + 64)
                pr = slice(hp * 64, hp * 64 + 64)
                tp = (0, 64) if hp == 1 else None
                self.mm(ps[ob][pr, col0:col0 + N], va, W[n % 3][:nk, hp, :N], t["first"], t["last"],
                        [vkey, "W%d" % (n % 3)], ["ps%d" % ob], tp=tp)
            if t["last"]:
                self.finish_sb(t, ob, oTt, oT_s)

        def df_A(t):
            kq_pair(t)

        def df_B(t):
            n, nk, col0, N = t["n"], t["nk"], t["col0"], t["N"]
            zp, key = pp[n % 2], "psz%d" % (n % 2)
            self.act(W[n % 3][:nk, :, :N], zp[:nk, :, col0:col0 + N], AF.Exp, [key], ["W%d" % (n % 3)])

        def df_C(t):
            n, nk, col0, N = t["n"], t["nk"], t["col0"], t["N"]
            va, vkey = vap(t, 0, 128)
            wt, wk = W[n % 3], "W%d" % (n % 3)
            for hp in range(2):
                obk = 4 + hp
                self.mm(ps[obk][:, col0:col0 + N], va, wt[:nk, hp, :N], t["first"], t["last"], [vkey, wk],
                        ["ps%d" % obk])
            for hp in range(2):
                dbk = 6 + hp
                self.mm(ps[dbk][:, col0:col0 + N], self.ones_b[:nk, :], wt[:nk, hp, :N], t["first"], t["last"],
                        ["cb", wk], ["ps%d" % dbk])
            if t["last"]:
                nq = t["nq"]
                tb = obank(t)
                nlam = self.lam[:, j, 3:4]
                gp = self.lam[:, j, 5:6]
                msb, mkey = pp[0][:, 0, :], "psz0"
                self.act(recp[:, :, :nq], pp[3][:, :, :nq], AF.Ln, ["ps6", "ps7"], ["rec"])
                self.act(recp[:, :, :nq], recp[:, :, :nq], AF.Exp, ["rec"], ["rec"], scale=-1.0)
                self.tt("dve", o0[:, :nq], ps[4][:, :nq], recp[:, 0, :nq], ALU.mult, ["ps4", "rec"], ["o0"])
                self.tt("dve", o1[:, :nq], ps[5][:, :nq], recp[:, 1, :nq], ALU.mult, ["ps5", "rec"], ["o1"])
                self.stt("dve", o0[:, :nq], o1[:, :nq], nlam, o0[:, :nq], ALU.mult, ALU.add, ["o0", "o1", "lam"], ["o0"])
                self.tt("dve", osq[:, :nq], o0[:, :nq], o0[:, :nq], ALU.mult, ["o0"], ["osq"])
                col = (self.SP if t["is_s"] else t["q0"])
                cc_ = t["c"]

                def part2():
                    self.mm(msb[:, :nq], self.onesm, osq[:, :nq], True, True, ["cb", "osq"], [mkey])
                    self.act(lnv[:, :nq], msb[:, :nq], AF.Ln, [mkey], ["lnv"], bias=SUBLN_EPS)
                    self.act(lnv[:, :nq], lnv[:, :nq], AF.Exp, ["lnv"], ["lnv"], scale=-0.5)
                    self.stt("dve", oTt[tb][:, :nq], o0[:, :nq], gp, lnv[:, :nq], ALU.mult, ALU.mult,
                             ["o0", "lnv", "lam"], ["oTt%d" % tb])
                    self.dma(oT_s[cc_][:, col:col + nq], oTt[tb][:, :nq], reads=["oTt%d" % tb])
                fin2.append((n + 2, part2))

        nt_ = len(tasks)
        fin2 = []
        self.alloc_staging(A)
        if l == 0:
            ncast = min(len(self.pending_casts), 18 + 26)
        else:
            ncast = min(len(self.pending_casts), 26)
        interval = max(1, (nt_ - 8) // (ncast + 1))
        self.deferred_store = None
        self.casts_left = ncast

        def do_cast_step():
            if self.deferred_store is not None:
                self.deferred_store()
                self.deferred_store = None
            if self.casts_left > 0 and self.pending_casts:
                self.casts_left -= 1
                self.deferred_store = self.pending_casts.pop(0)()
        load_chunk(0)
        cur_c = -1
        first_i = 0
        for i in range(nt_ + 2):
            if i < nt_:
                t = tasks[i]
                if t["c"] != cur_c:
                    cur_c = t["c"]
                    first_i = i
                if i == first_i + 2:
                    if cur_c + 1 < NCH:
                        load_chunk(cur_c + 1)
                if i >= 4 and i % interval == 0:
                    do_cast_step()
                if diff:
                    df_A(t)
                    df_B(t)
                else:
                    sb_A(t)
                    sb_B(t)
            if 0 <= i - 1 < nt_:
                t = tasks[i - 1]
                if diff:
                    df_C(t)
                    while fin2 and fin2[0][0] <= t["n"]:
                        fin2.pop(0)[1]()
                else:
                    sb_C(t)
                    sb_D(t)
                    sb_E(t)
            if not diff and 0 <= i - 2 < nt_:
                sb_F(tasks[i - 2])
        while fin2:
            fin2.pop(0)[1]()
        while self.casts_left > 0 and self.pending_casts:
            do_cast_step()
        do_cast_step()
        self.S.barrier()

    def finish_sb(self, t, ob, oTt, oT_s):
        nq = t["nq"]
        tb = ob - 6
        self.cp("dve", oTt[tb][:, :nq], self.ps[ob][:, :nq], ["ps%d" % ob], ["oTt%d" % tb])
        col = (self.SP if t["is_s"] else t["q0"])
        self.dma(oT_s[t["c"]][:, col:col + nq], oTt[tb][:, :nq], reads=["oTt%d" % tb])

    def ln_gen(self, l, which, R, Rk, N, out_bf, out_key, vbt, sqt, sm):
        ps = self.ps
        self.cp("act", vbt[:, :, :N], R[:, :, :N], [Rk], ["vbt"]); yield
        self.act(sqt[:, :, :N], R[:, :, :N], AF.Square, [Rk], ["sqt"]); yield
        for kc in range(NCH):
            self.mm(ps[4][:, :N], self.ones1024[:], vbt[:, kc, :N], kc == 0, kc == NCH - 1, ["cb", "vbt"], ["ps4"])
        for kc in range(NCH):
            self.mm(ps[5][:, :N], self.ones1024[:], sqt[:, kc, :N], kc == 0, kc == NCH - 1, ["cb", "sqt"], ["ps5"])
        yield
        mean, m2, rstd, nmr = sm[0], sm[1], sm[2], sm[3]
        self.cp("dve", mean[:, :N], ps[4][:, :N], ["ps4"], ["sm0"]); yield
        self.tt("dve", m2[:, :N], mean[:, :N], mean[:, :N], ALU.mult, ["sm0"], ["sm1"])
        self.tt("dve", m2[:, :N], ps[5][:, :N], m2[:, :N], ALU.subtract, ["ps5", "sm1"], ["sm1"]); yield
        self.act(rstd[:, :N], m2[:, :N], AF.Ln, ["sm1"], ["sm2"], bias=LN_EPS)
        self.act(rstd[:, :N], rstd[:, :N], AF.Exp, ["sm2"], ["sm2"], scale=-0.5); yield
        self.stt("dve", nmr[:, :N], mean[:, :N], -1.0, rstd[:, :N], ALU.mult, ALU.mult, ["sm0", "sm2"], ["sm3"])
        yield
        gi = 0 if which == 1 else 2
        g = self.lnp[:, l, gi, :]
        b = self.lnp[:, l, gi + 1, :]
        half = NCH // 2
        for h in range(2):
            Rh = R[:, h * half:(h + 1) * half, :N]
            bc = lambda a: a[:, :N].unsqueeze(1).to_broadcast([128, half, N])
            self.tt("dve", Rh, Rh, bc(rstd), ALU.mult, [Rk, "sm2"], [Rk]); yield
            self.tt("dve", Rh, Rh, bc(nmr), ALU.add, [Rk, "sm3"], [Rk]); yield
            for kc in range(h * half, (h + 1) * half):
                self.act(R[:, kc, :N], R[:, kc, :N], AF.Identity, [Rk, "lnp"], [Rk],
                         scale=g[:, kc:kc + 1], bias=b[:, kc:kc + 1])
                if kc % 2 == 1:
                    yield
            self.cp("dve", out_bf[:, h * half:(h + 1) * half, :N], Rh, [Rk], [out_key]); yield

    def phase3(self, l):
        nc, SP = self.nc, self.SP
        last_layer = (l == self.NL - 1)
        A = Alloc(nc, self.persist_end)
        NR = 3
        ring = [A.t("ring", [128, 4096], BF16) for _ in range(NR)]
        oT = A.t("oT", [128, NCH, 512], BF16)
        Rb = [A.t("R", [128, NCH, 512], F32) for _ in range(2)]
        xbf = A.t("xbf", [128, NCH, 512], BF16)
        vbt = A.t("vbt", [128, NCH, 512], BF16)
        sqt = A.t("sqt", [128, NCH, 512], BF16)
        hT = A.t("hT", [128, 32, 512], BF16)
        sm = [A.t("sm", [128, 512], F32) for _ in range(4)]
        rl = [A.t("rl", [128, 512], F32) for _ in range(2)]
        yt = [A.t("yt", [128, D], F32) for _ in range(2)] if last_layer else None
        ps = self.ps
        xT_f = self.xT_f.rearrange("c p t -> p c t")
        xT_b = self.xT_b.rearrange("c p t -> p c t")
        oT_s = self.oT_s.rearrange("c p t -> p c t")
        groups = [(g * 512, 512) for g in range(SP // 512)] + [(SP, DS)]
        NG = len(groups)

        seq = [("wo", 0, 0), ("wo", 0, 1)]
        for gi in range(NG):
            seq += [("wu", gi, fb) for fb in range(8)]
            seq += [("wd", gi, 0), ("wd", gi, 1)]
            if gi + 1 < NG:
                seq += [("wo", gi + 1, 0), ("wo", gi + 1, 1)]
            seq += [("wd", gi, oc) for oc in range(2, 8)]
        pos_of = {k: i for i, k in enumerate(seq)}
        loaded = {}
        st = {"pos": 0, "acc": 0, "hc": 0}
        PRE = NR - 1

        def wsrc(kind, i):
            if kind == "wo":
                return self.wo_b[l, i].rearrange("p k c -> p (k c)"), ["wo%d_%d" % (l, i)]
            if kind == "wu":
                return self.wup_b[l, i].rearrange("p k c -> p (k c)"), ["wu%d_%d" % (l, i)]
            return self.wdn_b[l, i].rearrange("p f c -> p (f c)"), ["wd%d_%d" % (l, i)]

        def get_w(key):
            pos = pos_of[key]
            while st["pos"] <= pos + PRE and st["pos"] < len(seq):
                kind, _, i = seq[st["pos"]]
                src, rk = wsrc(kind, i)
                sl = st["pos"] % NR
                self.dma(ring[sl][:, :], src, reads=rk, writes=["ring%d" % sl])
                loaded[st["pos"]] = (ring[sl], "ring%d" % sl)
                st["pos"] += 1
            return loaded.pop(pos)

        def wo_part(gi):
            g0, N = groups[gi]
            R, Rk = Rb[gi % 2], "R3_%d" % (gi % 2)
            self.dma(oT[:, :, :N], oT_s[:, :, g0:g0 + N], writes=["oT"])
            self.dma(R[:, :, :N], xT_f[:, :, g0:g0 + N], writes=[Rk])
            for ob in range(2):
                wt, wk = get_w(("wo", gi, ob))
                w3 = wt[:, :].rearrange("p (k c) -> p k c", c=512)
                for o4 in range(4):
                    oc = ob * 4 + o4
                    bk = st["acc"] % 2
                    st["acc"] += 1
                    for kc in range(NCH):
                        self.mm(ps[bk][:, :N], w3[:, kc, o4 * 128:(o4 + 1) * 128], oT[:, kc, :N],
                                kc == 0, kc == NCH - 1, [wk, "oT"], ["ps%d" % bk])
                    self.stt("dve", R[:, oc, :N], R[:, oc, :N], ALPHA, ps[bk][:, :N], ALU.mult, ALU.add,
                             ["ps%d" % bk, Rk], [Rk])

        def drain(gen, k=None):
            if gen is None:
                return None
            try:
                if k is None:
                    while True:
                        next(gen)
                else:
                    for _ in range(k):
                        next(gen)
            except StopIteration:
                return None
            return gen

        def finish_group(gi):
            g0, N = groups[gi]
            R, Rk = Rb[gi % 2], "R3_%d" % (gi % 2)
            if not last_layer:
                self.dma(xT_f[:, :, g0:g0 + N], R[:, :, :N], reads=[Rk])
                self.dma(xT_b[:, :, g0:g0 + N], vbt[:, :, :N], reads=["vbt"])
            else:
                ntile = max(1, N // 128)
                for ti in range(ntile):
                    nt = min(128, N)
                    c0 = ti * 128
                    yb = ti % 2
                    for half in range(2):
                        bk = 6 + half
                        for cc in range(4):
                            c = half * 4 + cc
                            self.tr(ps[bk][:nt, cc * 128:(cc + 1) * 128], R[:, c, c0:c0 + nt], self.ident_f,
                                    [Rk, "cf"], ["ps%d" % bk])
                        self.cp("dve" if half == 0 else "act", yt[yb][:nt, half * 512:(half + 1) * 512],
                                ps[bk][:nt, :], ["ps%d" % bk], ["yt%d" % yb])
                    if N == DS:
                        self.dma(self.ys[0:nt, :], yt[yb][:nt, :], reads=["yt%d" % yb])
                    else:
                        self.dma(self.yp[g0 + c0:g0 + c0 + nt, :], yt[yb][:nt, :], reads=["yt%d" % yb])

        ln2_prev = None
        wo_part(0)
        drain(self.ln_gen(l, 1, Rb[0], "R3_0", groups[0][1], xbf, "xbf", vbt, sqt, sm))
        for gi, (g0, N) in enumerate(groups):
            R, Rk = Rb[gi % 2], "R3_%d" % (gi % 2)
            for fb in range(8):
                wt, wk = get_w(("wu", gi, fb))
                w3 = wt[:, :].rearrange("p (k c) -> p k c", c=512)
                for f4 in range(4):
                    f = fb * 4 + f4
                    bk = 2 + st["hc"] % 2
                    st["hc"] += 1
                    for kc in range(NCH):
                        self.mm(ps[bk][:, :N], w3[:, kc, f4 * 128:(f4 + 1) * 128], xbf[:, kc, :N],
                                kc == 0, kc == NCH - 1, [wk, "xbf"], ["ps%d" % bk])
                    rb = f % 2
                    self.act(rl[rb][:, :N], ps[bk][:, :N], AF.Relu, ["ps%d" % bk], ["rl%d" % rb])
                    self.tt("pool", hT[:, f, :N], rl[rb][:, :N], rl[rb][:, :N], ALU.mult, ["rl%d" % rb], ["hT%d" % f])
                    if f >= 2:
                        ln2_prev = drain(ln2_prev, 1)
            if gi > 0:
                drain(ln2_prev)
                ln2_prev = None
                finish_group(gi - 1)
            ln1_next = None
            for oc in range(8):
                wt, wk = get_w(("wd", gi, oc))
                w3 = wt[:, :].rearrange("p (f c) -> p f c", c=128)
                bk = st["acc"] % 2
                st["acc"] += 1
                for f in range(32):
                    self.mm(ps[bk][:, :N], w3[:, f, :], hT[:, f, :N], f == 0, f == 31, [wk, "hT%d" % f], ["ps%d" % bk])
                self.stt("dve", R[:, oc, :N], R[:, oc, :N], ALPHA, ps[bk][:, :N], ALU.mult, ALU.add,
                         ["ps%d" % bk, Rk], [Rk])
                if oc == 1 and gi + 1 < NG:
                    wo_part(gi + 1)
                    ln1_next = self.ln_gen(l, 1, Rb[(gi + 1) % 2], "R3_%d" % ((gi + 1) % 2), groups[gi + 1][1],
                                           xbf, "xbf", vbt, sqt, sm)
                elif oc >= 2:
                    ln1_next = drain(ln1_next, 5)
            drain(ln1_next)
            ln2_prev = self.ln_gen(l, 2, R, Rk, N, vbt, "vbt", vbt, sqt, sm)
        drain(ln2_prev)
        finish_group(NG - 1)
        self.S.barrier()

    def build(self):
        self.marks = []
        mk = lambda name: self.marks.append((name, len(self.S.ops["pe"])))
        self.declare()
        self.prologue()
        mk("prologue")
        self.load_wqkv(0)
        sa = getattr(self, "stop_after", None)
        for l in range(self.NL):
            self.phase1(l)
            mk("L%d.qkv" % l)
            if sa == (l, 1):
                break
            self.phase2(l)
            mk("L%d.attn" % l)
            if sa == (l, 2):
                break
            if l + 1 < self.NL:
                self.load_wqkv(l + 1)
            self.phase3(l)
            mk("L%d.mlp" % l)
            if sa == (l, 3):
                break
        self.S.emit(self.nc)
        return self.nc


def make_consts():
    c = np.zeros((128, 7, 128), np.float32)
    p = np.arange(128)[:, None]
    i = np.arange(128)[None, :]
    c[:, 0] = (p == i)
    c[:, 1] = np.where(p >= i, -1.0, 0.0)
    c[:, 2] = -1.0
    c[:, 3] = 1.0
    c[:, 4] = np.where(p < i, 0.0, -10000.0)
    c[:, 5] = np.where((p // 64) <= (i // 64), 0.0, -10000.0)
    c[:, 6] = 1.0 / 128.0
    return c.reshape(128, 7 * 128)


def make_rope(SP, SC):
    pos = np.concatenate([np.arange(SP), SC + np.arange(DS)]).astype(np.float32)
    inv = (np.float32(ROPE_THETA) ** (-np.arange(0, 16, 2, dtype=np.float32) / np.float32(16))).astype(np.float32)
    ang = (pos[:, None] * inv[None, :]).astype(np.float32)
    return np.concatenate([np.cos(ang), np.sin(ang)], axis=1).astype(np.float32)


_CACHE = {}


def run(inputs, SP, SC, NL, n_cores, same_engine_sync=True, debug=False, stop_after=None):
    key = (SP, SC, NL, same_engine_sync, debug, stop_after)
    if key not in _CACHE:
        bld = Builder(SP, SC, NL, same_engine_sync, debug)
        bld.stop_after = stop_after
        _CACHE[key] = bld.build()
    nc = _CACHE[key]
    f = lambda a: np.ascontiguousarray(np.asarray(a, dtype=np.float32))
    consts = make_consts()
    rope = make_rope(SP, SC)
    shared = {
        "sb_wqkv": f(inputs["sb_w_qkv"]), "sb_wo": f(inputs["sb_w_o"]),
        "df_wqkv": f(inputs["diff_w_qkv"]), "df_wo": f(inputs["diff_w_o"]),
        "lq1": f(inputs["diff_lambda_q1"]), "lk1": f(inputs["diff_lambda_k1"]),
        "lq2": f(inputs["diff_lambda_q2"]), "lk2": f(inputs["diff_lambda_k2"]),
        "subg": f(inputs["diff_subln_g"]),
        "ln1g": f(inputs["ln1_g"]), "ln1b": f(inputs["ln1_b"]),
        "ln2g": f(inputs["ln2_g"]), "ln2b": f(inputs["ln2_b"]),
        "wup": f(inputs["mlp_w_up"]), "wdn": f(inputs["mlp_w_down"]),
        "consts": consts, "rope": rope,
    }
    in_maps = []
    for b in range(n_cores):
        m = dict(shared)
        m["xp"] = f(inputs["x_prompt"][b])
        m["xs"] = f(inputs["x_sample"][b])
        m["csbk"] = f(np.asarray(inputs["cache_sb_k"])[:, b]).reshape(2, SC, D)
        m["csbv"] = f(np.asarray(inputs["cache_sb_v"])[:, b]).reshape(2, SC, D)
        m["cdfk"] = f(np.asarray(inputs["cache_diff_k"])[:, b]).reshape(2, SC, D)
        m["cdfv"] = f(np.asarray(inputs["cache_diff_v"])[:, b]).reshape(2, SC, D)
        in_maps.append(m)
    res = run_bass_kernel_spmd(nc, in_maps, core_ids=list(range(n_cores)))
    R = res.results
    if debug:
        return R
    st = lambda k, ax: np.stack([np.asarray(r[k]) for r in R], axis=ax)
    y_p = st("yp", 0)
    y_s = st("ys", 0)
    nsb = 2
    sbk_p = st("o_sbk_p", 1).reshape(nsb, n_cores, SP, 16, 64)
    sbv_p = st("o_sbv_p", 1).reshape(nsb, n_cores, SP, 16, 64)
    dfk_p = st("o_dfk_p", 1).reshape(2, n_cores, SP, 8, 2, 64)
    dfv_p = st("o_dfv_p", 1).reshape(2, n_cores, SP, 8, 128)
    sbk_s = st("o_sbk_s", 1).reshape(nsb, n_cores, DS, 16, 64)
    sbv_s = st("o_sbv_s", 1).reshape(nsb, n_cores, DS, 16, 64)
    dfk_s = st("o_dfk_s", 1).reshape(2, n_cores, DS, 8, 2, 64)
    dfv_s = st("o_dfv_s", 1).reshape(2, n_cores, DS, 8, 128)
    return (y_p, y_s, sbk_p, sbv_p, dfk_p, dfv_p, sbk_s, sbv_s, dfk_s, dfv_s)


def kernel(**inputs):
    return run(inputs, 4096, 2048, 4, 8)
```
